# Optimizing a Trainium2 kernel written in Bass

```python
import math
import jax
import jax.numpy as jnp
from jax import lax
import numpy as np

D_MODEL = 4096
BATCH = 4
SEQ = 4096
DEPTH = 2

MIX = D_MODEL // 2
HEAD_DIM = 128
N_HEADS = MIX // HEAD_DIM
IN_COLS = 5 * MIX
CONV_K = 31
DSW_PATTERNS = ((128, 1), (512, 4), (2048, 16))
NUM_BUCKETS = 32
MAX_DISTANCE = 2048
HGRN_CHUNK = 64
POOL_WINDOWS = (2, 4, 8, 16)
POOL_GROUP = MIX // len(POOL_WINDOWS)
D_FF = 4 * D_MODEL
ALPHA = (2.0 * DEPTH) ** 0.25
BETA = (8.0 * DEPTH) ** -0.25
N_EVEN = (DEPTH + 1) // 2
N_ODD = DEPTH // 2
LN_EPS = 1e-5

kernel_name = 'hybrid_conv_dilattn_hgrn2_pool_block'


def layer_norm(x, g, b):
    xf = x.astype(jnp.float32)
    mu = jnp.mean(xf, axis=-1, keepdims=True)
    var = jnp.mean(jnp.square(xf - mu), axis=-1, keepdims=True)
    y = (xf - mu) * lax.rsqrt(var + LN_EPS)
    return (y * g.astype(jnp.float32) + b.astype(jnp.float32)).astype(x.dtype)


def t5_bucket(dist):
    max_exact = NUM_BUCKETS // 2
    nf = jnp.maximum(dist, 1).astype(jnp.float32)
    large = max_exact + (jnp.log(nf / max_exact) / math.log(MAX_DISTANCE / max_exact)
                         * (NUM_BUCKETS - max_exact)).astype(jnp.int32)
    large = jnp.minimum(large, NUM_BUCKETS - 1)
    return jnp.where(dist < max_exact, dist, large)


def conformer_conv(a_val, a_gate, conv_w, conv_b, ln_g, ln_b):
    u = a_val * jax.nn.sigmoid(a_gate)
    u = lax.conv_general_dilated(
        u, conv_w[:, None, :].astype(u.dtype), (1,), [(CONV_K - 1, 0)],
        dimension_numbers=('NWC', 'WIO', 'NWC'), feature_group_count=u.shape[-1])
    u = u + conv_b.astype(u.dtype)
    return jax.nn.silu(layer_norm(u, ln_g, ln_b))


def dilated_group(q, k, v, rel_bias, window, dil):
    bsz, seq, heads, hd = q.shape
    steps = window // dil
    sub_len = seq // dil
    nb = -(-sub_len // steps)
    pad_len = nb * steps

    def strided(t):
        t = t.reshape(bsz, sub_len, dil, heads, hd)
        return jnp.pad(t, ((0, 0), (0, pad_len - sub_len), (0, 0), (0, 0), (0, 0)))

    def band(t):
        t = jnp.pad(strided(t), ((0, 0), (steps, 0), (0, 0), (0, 0), (0, 0)))
        t = t.reshape(bsz, nb + 1, steps, dil, heads, hd)
        return jnp.concatenate([t[:, :-1], t[:, 1:]], axis=2)

    qb = strided(q).reshape(bsz, nb, steps, dil, heads, hd)
    kb, vb = band(k), band(v)
    qi = jnp.arange(steps)[:, None]
    ki = jnp.arange(2 * steps)[None, :]
    step = qi + steps - ki
    bias = rel_bias.astype(jnp.float32)[t5_bucket(jnp.clip(step, 0, steps) * dil)]
    bias = bias.transpose(2, 0, 1)
    key_pos = jnp.arange(nb)[:, None, None] * steps + ki[None] - steps
    valid = (step >= 0) & (step <= steps) & (key_pos >= 0)
    s = jnp.einsum('bnqrhe,bnkrhe->bnrhqk', qb, kb) * (hd ** -0.5) + bias
    s = jnp.where(valid[None, :, None, None], s, -jnp.inf)
    m = jnp.max(s, axis=-1, keepdims=True)
    p = jnp.exp(s - m)
    l = jnp.sum(p, axis=-1)
    o = jnp.einsum('bnrhqk,bnkrhe->bnqrhe', p, vb) / l.transpose(0, 1, 4, 2, 3)[..., None]
    lse = (m[..., 0] + jnp.log(l)).transpose(0, 1, 4, 2, 3)
    o = o.reshape(bsz, pad_len, dil, heads, hd)[:, :sub_len].reshape(bsz, seq, heads, hd)
    lse = lse.reshape(bsz, pad_len, dil, heads)[:, :sub_len].reshape(bsz, seq, heads)
    return o, lse


def dilated_attention(q, k, v, rel_bias):
    q, k, v = (t.astype(jnp.float32) for t in (q, k, v))
    res = [dilated_group(q, k, v, rel_bias, w, d) for (w, d) in DSW_PATTERNS]
    o = jnp.stack([r[0] for r in res])
    lse = jnp.stack([r[1] for r in res])
    wts = jax.nn.softmax(lse, axis=0)
    return jnp.sum(wts[..., None] * o, axis=0)


def hgrn2(q, f_pre, i, g_out, lb, norm_g):
    bsz, seq, _ = q.shape
    dt = q.dtype
    q = jax.nn.silu(q.astype(jnp.float32)).reshape(bsz, seq, N_HEADS, HEAD_DIM)
    f = lb + (1.0 - lb) * jax.nn.sigmoid(f_pre.astype(jnp.float32))
    logf = jnp.log(f).reshape(bsz, seq, N_HEADS, HEAD_DIM)
    kk = (1.0 - f).reshape(bsz, seq, N_HEADS, HEAD_DIM)
    v = i.astype(jnp.float32).reshape(bsz, seq, N_HEADS, HEAD_DIM)
    n_chunks = seq // HGRN_CHUNK

    def to_chunks(t):
        return t.reshape(bsz, n_chunks, HGRN_CHUNK, N_HEADS, HEAD_DIM).transpose(1, 0, 3, 2, 4)

    causal = jnp.tril(jnp.ones((HGRN_CHUNK, HGRN_CHUNK), dtype=bool))

    def step(state, inp):
        qc, kc, vc, gc = inp
        b = jnp.cumsum(gc, axis=2)
        inter = jnp.einsum('bhtk,bhkv->bhtv', qc * jnp.exp(b), state)
        diff = b[:, :, :, None, :] - b[:, :, None, :, :]
        decay = jnp.exp(jnp.where(causal[None, None, :, :, None], diff, -jnp.inf))
        attn = jnp.einsum('bhtk,bhtsk,bhsk->bhts', qc, decay, kc)
        o = inter + jnp.einsum('bhts,bhsv->bhtv', attn, vc)
        b_last = b[:, :, -1:, :]
        state = (jnp.exp(b_last[:, :, 0, :, None]) * state
                 + jnp.einsum('bhsk,bhsv->bhkv', kc * jnp.exp(b_last - b), vc))
        return state, o

    s0 = jnp.zeros((bsz, N_HEADS, HEAD_DIM, HEAD_DIM), jnp.float32)
    _, o = lax.scan(step, s0, (to_chunks(q), to_chunks(kk), to_chunks(v), to_chunks(logf)))
    o = o.transpose(1, 0, 3, 2, 4).reshape(bsz, seq, N_HEADS, HEAD_DIM)
    o = o * lax.rsqrt(jnp.mean(jnp.square(o), axis=-1, keepdims=True) + LN_EPS)
    o = o * norm_g.astype(jnp.float32).reshape(N_HEADS, HEAD_DIM)
    o = o.reshape(bsz, seq, MIX) * jax.nn.silu(g_out.astype(jnp.float32))
    return o.astype(dt)


def multiscale_pool(p, pool_w, pool_scale):
    bsz, seq, _ = p.shape
    pg = p.astype(jnp.float32).reshape(bsz, seq, len(POOL_WINDOWS), POOL_GROUP)
    cs0 = jnp.pad(jnp.cumsum(pg, axis=1), ((0, 0), (1, 0), (0, 0), (0, 0)))
    t = jnp.arange(seq)
    outs = []
    for gi, w in enumerate(POOL_WINDOWS):
        cg = cs0[:, :, gi]
        lag = jnp.pad(cg[:, :seq], ((0, 0), (w - 1, 0), (0, 0)))[:, :seq]
        mean = (cg[:, 1:] - lag) / jnp.minimum(t + 1, w).astype(jnp.float32)[None, :, None]
        outs.append(mean - pg[:, :, gi])
    pooled = jnp.stack(outs, axis=2).astype(p.dtype)
    y = jnp.einsum('bsgc,gcd->bsgd', pooled, pool_w).reshape(bsz, seq, MIX)
    return (y * pool_scale).astype(p.dtype)


def even_mixer(h, w_in, w_out, conv_w, conv_b, conv_ln_g, conv_ln_b, rel_bias):
    bsz, seq, _ = h.shape
    u = h @ w_in
    a_val, a_gate, q, k, v = jnp.split(u, 5, axis=-1)
    a_out = conformer_conv(a_val, a_gate, conv_w, conv_b, conv_ln_g, conv_ln_b)
    heads = lambda t: t.reshape(bsz, seq, N_HEADS, HEAD_DIM)
    b_out = dilated_attention(heads(q), heads(k), heads(v), rel_bias).reshape(bsz, seq, MIX)
    return jnp.concatenate([a_out, b_out.astype(h.dtype)], axis=-1) @ w_out


def odd_mixer(h, w_in, w_out, lb, norm_g, pool_w, pool_scale):
    u = h @ w_in
    cq, cf, ci, cg, dp = jnp.split(u, 5, axis=-1)
    c_out = hgrn2(cq, cf, ci, cg, lb, norm_g)
    d_out = multiscale_pool(dp, pool_w, pool_scale)
    return jnp.concatenate([c_out, d_out], axis=-1) @ w_out


def setup_inputs(seed: int = 0) -> dict:
    key = jax.random.key(seed)
    ks = jax.random.split(key, 20)
    nrm = lambda k, shape, s: jax.random.normal(k, shape, jnp.float32) * s
    return {
        'x': nrm(ks[0], (BATCH, SEQ, D_MODEL), 1.0),
        'c': nrm(ks[1], (BATCH, D_MODEL), 1.0),
        'ada_w': nrm(ks[2], (DEPTH, D_MODEL, 6 * D_MODEL), 0.5 * D_MODEL ** -0.5),
        'ada_b': nrm(ks[3], (DEPTH, 6 * D_MODEL), 0.01),
        'w_in': nrm(ks[4], (DEPTH, D_MODEL, IN_COLS), D_MODEL ** -0.5),
        'w_out': nrm(ks[5], (DEPTH, 2 * MIX, D_MODEL), BETA * (2 * MIX) ** -0.5),
        'ln_g': 1.0 + nrm(ks[6], (DEPTH, 2, D_MODEL), 0.05),
        'ln_b': nrm(ks[7], (DEPTH, 2, D_MODEL), 0.02),
        'mlp_w1': nrm(ks[8], (DEPTH, D_MODEL, D_FF), BETA * D_MODEL ** -0.5),
        'mlp_w2': nrm(ks[9], (DEPTH, D_FF, D_MODEL), BETA * D_FF ** -0.5),
        'conv_w': nrm(ks[10], (N_EVEN, CONV_K, MIX), CONV_K ** -0.5),
        'conv_b': nrm(ks[11], (N_EVEN, MIX), 0.02),
        'conv_ln_g': 1.0 + nrm(ks[12], (N_EVEN, MIX), 0.05),
        'conv_ln_b': nrm(ks[13], (N_EVEN, MIX), 0.02),
        'rel_bias': nrm(ks[14], (NUM_BUCKETS, N_HEADS), 0.5),
        'hgrn_lb_logits': nrm(ks[15], (DEPTH, MIX), 0.5),
        'hgrn_norm_g': 1.0 + nrm(ks[16], (N_ODD, MIX), 0.05),
        'pool_w': nrm(ks[17], (N_ODD, len(POOL_WINDOWS), POOL_GROUP, POOL_GROUP), POOL_GROUP ** -0.5),
        'pool_scale': 1.0 + nrm(ks[18], (N_ODD, MIX), 0.1),
    }


def reference(x, c, ada_w, ada_b, w_in, w_out, ln_g, ln_b, mlp_w1, mlp_w2,
              conv_w, conv_b, conv_ln_g, conv_ln_b, rel_bias,
              hgrn_lb_logits, hgrn_norm_g, pool_w, pool_scale):
    lb_all = jnp.cumsum(jax.nn.softmax(hgrn_lb_logits.astype(jnp.float32), axis=0), axis=0)
    lb_all = lb_all - lb_all[0]
    cs = jax.nn.silu(c)
    for l in range(DEPTH):
        mod = cs @ ada_w[l] + ada_b[l]
        sh1, sc1, g1, sh2, sc2, g2 = jnp.split(mod[:, None, :], 6, axis=-1)
        h = x * (1.0 + sc1) + sh1
        if l % 2 == 0:
            e = l // 2
            y = even_mixer(h, w_in[l], w_out[l], conv_w[e], conv_b[e],
                           conv_ln_g[e], conv_ln_b[e], rel_bias)
        else:
            o = l // 2
            y = odd_mixer(h, w_in[l], w_out[l], lb_all[l], hgrn_norm_g[o],
                          pool_w[o], pool_scale[o])
        x = layer_norm(ALPHA * x + g1 * y, ln_g[l, 0], ln_b[l, 0])
        h = x * (1.0 + sc2) + sh2
        y = jnp.square(jax.nn.relu(h @ mlp_w1[l])) @ mlp_w2[l]
        x = layer_norm(ALPHA * x + g2 * y, ln_g[l, 1], ln_b[l, 1])
    return x
```

```python
import numpy as np
from contextlib import ExitStack
import concourse.bass as bass
import concourse.mybir as mybir
from concourse.bass_utils import run_bass_kernel_spmd

F32 = mybir.dt.float32
BF16 = mybir.dt.bfloat16
AF = mybir.ActivationFunctionType
ALU = mybir.AluOpType

D = 4096
SEQ = 4096
NB = 4
T = 2048
TE = 4096
MIX = 2048
NCH = D // 128
ALPHA = 4.0 ** 0.25
EPS = 1e-5
CONV_K = 31
HALO = 128
SAME_ENG_SYNC = True
DEBUG_OUT = set()
LAST_RES = [None]


class Buf:
    __slots__ = ("name", "w", "r", "dsem", "excl", "rd")

    def __init__(self, name, excl=False):
        self.name = name
        self.excl = excl
        self.rd = None
        self.w = {}
        self.r = {}
        self.dsem = None


class Prog:
    ENGS = ("pe", "act", "dve", "pool", "sp")

    def __init__(self, nc, es, n_dma_sems=90):
        self.nc = nc
        self.ops = {e: [] for e in self.ENGS}
        self.sems = {}
        self.cnt = {}
        for e in self.ENGS:
            self.sems["E" + e] = es.enter_context(nc.semaphore("prog_" + e))
            self.cnt["E" + e] = 0
        self.free_dsems = []
        for i in range(n_dma_sems):
            k = "D%d" % i
            self.sems[k] = es.enter_context(nc.semaphore("dma_%d" % i))
            self.cnt[k] = 0
            self.free_dsems.append(k)
        self.sems["CC"] = es.enter_context(nc.semaphore("coll_cc"))
        self.cnt["CC"] = 0
        self.waited = {e: {} for e in self.ENGS}
        self.bufs = []

    def buf(self, name, excl=False):
        b = Buf(name, excl)
        self.bufs.append(b)
        return b

    def bufs_n(self, name, n, excl=False):
        return [self.buf("%s%d" % (name, i), excl) for i in range(n)]

    def pbuf(self, name):
        return self.buf(name, True)

    def pbufs_n(self, name, n):
        return self.bufs_n(name, n, True)

    def _waits(self, eng, reads, writes):
        need = {}
        for b in reads:
            for k, v in b.w.items():
                if b.excl and b.rd == eng and k == "E" + eng:
                    continue
                if need.get(k, 0) < v:
                    need[k] = v
            if b.excl:
                for k, v in b.r.items():
                    if need.get(k, 0) < v:
                        need[k] = v
        for b in writes:
            for k, v in b.w.items():
                if need.get(k, 0) < v:
                    need[k] = v
            for k, v in b.r.items():
                if need.get(k, 0) < v:
                    need[k] = v
        out = []
        wd = self.waited[eng]
        for k, v in need.items():
            if k == "E" + eng and (eng == "pe" or eng == "sp" or not SAME_ENG_SYNC):
                continue
            if wd.get(k, 0) >= v:
                continue
            wd[k] = v
            out.append((k, v))
        return out

    def op(self, eng, meth, *args, reads=(), writes=(), sig=True, **kw):
        waits = self._waits(eng, reads, writes)
        k = "E" + eng
        if not sig:
            self.ops[eng].append((waits, meth, args, kw, None))
            return
        self.cnt[k] += 1
        v = self.cnt[k]
        self.ops[eng].append((waits, meth, args, kw, (k, 1)))
        for b in writes:
            b.w = {k: v}
            b.r = {}
            b.rd = None
        for b in reads:
            if b.excl:
                if b not in writes:
                    b.w = {k: v}
                    b.r = {}
                    b.rd = eng
            elif b.r.get(k, 0) < v:
                b.r[k] = v

    def dma(self, q, out, in_, sb, dram, load, **kw):
        if sb.dsem is None:
            sb.dsem = self.free_dsems.pop()
        k = sb.dsem
        if load:
            reads, writes = ([dram] if dram is not None else []), [sb]
        else:
            reads, writes = [sb], ([dram] if dram is not None else [])
        waits = self._waits(q, reads, writes)
        self.cnt[k] += 16
        v = self.cnt[k]
        self.ops[q].append((waits, "dma_start", (), dict(out=out, in_=in_, **kw), (k, 16)))
        for b in writes:
            if b is sb:
                b.w = {k: v}
                b.r = {}
            else:
                b.w[k] = v
        for b in reads:
            if b.r.get(k, 0) < v:
                b.r[k] = v

    def coll(self, kind, alu, groups, in_ap, out_ap, b_in, b_out):
        k = "CC"
        if k not in self.sems:
            raise RuntimeError("no CC sem")
        waits = self._waits("pool", [b_in], [b_out])
        self.cnt[k] += 1
        v = self.cnt[k]
        self.ops["pool"].append((waits, "collective_compute", (kind, alu),
                                 dict(replica_groups=groups, ins=[in_ap], outs=[out_ap]), (k, 1)))
        b_out.w[k] = v
        if b_in.r.get(k, 0) < v:
            b_in.r[k] = v

    def barrier(self):
        for e in self.ENGS:
            waits = []
            wd = self.waited[e]
            for k, v in self.cnt.items():
                if v == 0 or k == "E" + e:
                    continue
                if wd.get(k, 0) >= v:
                    continue
                wd[k] = v
                waits.append((k, v))
            if waits:
                self.ops[e].append((waits, None, (), {}, None))
        for b in self.bufs:
            b.w = {}
            b.r = {}
            if b.dsem is not None:
                self.free_dsems.append(b.dsem)
                b.dsem = None
        self.bufs = [b for b in self.bufs if getattr(b, "keep", False) or True]

    def emit(self):
        nc = self.nc
        engmap = {"pe": "tensor", "act": "scalar", "dve": "vector", "pool": "gpsimd", "sp": "sync"}
        with nc.Block() as block:
            for e in self.ENGS:
                ops = self.ops[e]
                sems = self.sems

                def body(eng, ops=ops, sems=sems):
                    for waits, meth, args, kw, inc in ops:
                        for k, v in waits:
                            eng.wait_ge(sems[k], v)
                        if meth is None:
                            continue
                        ins = getattr(eng, meth)(*args, **kw)
                        if inc is not None:
                            ins.then_inc(sems[inc[0]], inc[1])

                getattr(block, engmap[e])(body)


class Ctx:
    pass


_UID = [0]


def sb(nc, es, name, shape, dt):
    _UID[0] += 1
    return es.enter_context(nc.sbuf_tensor("%s_u%d" % (name, _UID[0]), list(shape), dt))


def ps(nc, es, name, shape, dt=F32):
    _UID[0] += 1
    return es.enter_context(nc.psum_tensor("%s_u%d" % (name, _UID[0]), list(shape), dt))


PAIR_GROUPS = [[0, 1], [2, 3], [4, 5], [6, 7]]
XCH = 128


def build_prog(layers, phases=None):
    nc = bass.Bass("TRN2", target_bir_lowering=False)
    es = ExitStack()
    C = Ctx()
    C.nc = nc
    fused = len(layers) > 1
    dt_sc = lambda name, shape, dt=F32: nc.dram_tensor(
        name, list(shape), dt, kind=("ExternalOutput" if name in DEBUG_OUT else "Internal")).ap()
    if phases is None:
        phases = ["mod", "xT", "win", "mix", "wout", "ln0", "mlp1", "mlp2", "ln1"]
    C.declared = set()

    def dt_in(name, shape, dt=F32, ph=None):
        if ph is not None and not (set(ph) & set(phases)):
            return None
        C.declared.add(name)
        return nc.dram_tensor(name, list(shape), dt, kind="ExternalInput").ap()

    C.xe = dt_in("xe", [TE, D], ph=["xT"])
    C.flag = dt_in("flag", [128, 1])
    C.c_pf = dt_in("c_pf", [128, NCH])
    C.ident = dt_in("ident", [128, 128])
    C.y = nc.dram_tensor("y", [T, D], F32, kind="ExternalOutput").ap()
    C.S_xT = dt_sc("S_xT", [NCH, 128, T])
    C.S_hT = dt_sc("S_hT", [NCH, 128, TE], BF16)
    C.S_u32 = dt_sc("S_u32", [80, 128, TE])
    C.S_u16 = dt_sc("S_u16", [80, 128, TE], BF16)
    C.S_mix = dt_sc("S_mix", [NCH, 128, T], BF16)
    C.S_z = dt_sc("S_z", [NCH, 128, T])
    C.S_z2 = dt_sc("S_z2", [NCH, 128, T])
    C.S_h2 = dt_sc("S_h2", [NCH, 128, T], BF16)
    C.S_hid = dt_sc("S_hid", [128, 128, T], BF16)
    C.S_cv = dt_sc("S_cv", [16, 128, T])
    C.S_pl = dt_sc("S_pl", [16, 128, T], BF16)
    C.Mx = dt_sc("Mx", [128, 3 * NCH])
    C.Mg = dt_sc("Mg", [256, 3 * NCH])
    if fused:
        C.X1 = dt_sc("X1", [T, D])
        C.G = dt_sc("G", [T // XCH, 2 * XCH, D])

    P = Prog(nc, es)
    C.P = P
    for n in ("xT", "hT", "u", "mix", "z", "z2", "h2", "hid", "cv", "pl", "y", "x1", "G", "Mx", "Mg"):
        setattr(C, "B_" + n, P.buf("dram_" + n))

    C.ident_t = sb(nc, es, "ident_t", [128, 128], F32)
    C.identb_t = sb(nc, es, "identb_t", [128, 128], BF16)
    C.ones_t = sb(nc, es, "ones_t", [128, 128], F32)
    C.onesb_t = sb(nc, es, "onesb_t", [128, 128], BF16)
    C.flag_t = sb(nc, es, "flag_t", [128, 1], F32)
    C.mod_t = sb(nc, es, "mod_t", [128, 6 * NCH], F32)
    C.vec_t = sb(nc, es, "vec_t", [128, 10 * NCH], F32)
    C.lng_t = sb(nc, es, "lng_t", [128, 2 * NCH], F32)
    C.lnb_t = sb(nc, es, "lnb_t", [128, 2 * NCH], F32)
    C.B_const = P.buf("const")
    phase_setup(C)

    for li, layer in enumerate(layers):
        C.layer = layer
        sfx = str(layer) if fused else ""
        C.ada_w = dt_in("ada_w" + sfx, [D, 3 * D], ph=["mod"])
        C.ada_b = dt_in("ada_b" + sfx, [128, 3 * NCH])
        C.w_in = dt_in("w_in" + sfx, [D, 5 * MIX], ph=["win"])
        C.w_out = dt_in("w_out" + sfx, [D, D], ph=["wout"])
        C.w1 = dt_in("w1" + sfx, [D, 4 * D], ph=["mlp1"])
        C.w2 = dt_in("w2" + sfx, [4 * D, D], ph=["mlp2"])
        C.ln_g = dt_in("ln_g" + sfx, [128, 2 * NCH])
        C.ln_b = dt_in("ln_b" + sfx, [128, 2 * NCH])
        if layer == 0:
            C.conv_w = dt_in("conv_w", [128, 16 * CONV_K])
            C.conv_b = dt_in("conv_b", [128, 16])
            C.conv_g = dt_in("conv_g", [128, 16])
            C.conv_bb = dt_in("conv_bb", [128, 16])
            C.biasmat = dt_in("biasmat", [16, 128, 3 * 256], ph=["mix"])
        else:
            C.lbl = dt_in("lbl", [128, 2 * 16])
            C.hng = dt_in("hng", [128, 16])
            C.pool_w = dt_in("pool_w", [4, 512, 512], ph=["mix"])
            C.pool_s = dt_in("pool_s", [128, 16])
            C.invcnt = dt_in("invcnt", [128, 4 * T], ph=["mix"])
            C.cmask = dt_in("cmask", [64, 64])
        first, last = (li == 0), (li == len(layers) - 1)
        if first:
            C.xe_rows = lambda r0: (C.xe[r0:r0 + 128, :], None)
        else:
            C.xe_rows = lambda r0: ((C.G[r0 // XCH][0:128, :], C.B_G) if r0 < T
                                    else (C.X1[r0 - T:r0 - T + 128, :], C.B_x1))
        if last:
            C.out_rows = lambda r0: (C.y[r0:r0 + 128, :], C.B_y)
        else:
            C.out_rows = lambda r0: (C.X1[r0:r0 + 128, :], C.B_x1)
        bc = C.B_const
        P.dma("sp", C.lng_t[:], C.ln_g[:, :], bc, None, True)
        P.dma("sp", C.lnb_t[:], C.ln_b[:, :], bc, None, True)
        P.barrier()
        if "mod" in phases:
            phase_mod(C)
        if "xT" in phases:
            phase_xT(C)
        if "win" in phases:
            phase_win(C)
        if "mix" in phases:
            if layer == 0:
                phase_conv(C)
                phase_attn(C)
            else:
                phase_hgrn(C)
                phase_pool(C)
        if "wout" in phases:
            phase_wout(C)
        if "ln0" in phases:
            phase_ln(C, which=0)
        if "mlp1" in phases:
            phase_mlp1(C)
        if "mlp2" in phases:
            phase_mlp2(C)
        if "ln1" in phases:
            phase_ln(C, which=1)
        if not last:
            phase_exchange(C)
    P.barrier()
    P.emit()
    es.close()
    _DECLS[id(nc)] = C.declared
    return nc


def build_layer(layer, phases=None):
    return build_prog((layer,), phases)


def phase_exchange(C):
    P = C.P
    P.barrier()
    for i in range(T // XCH):
        P.coll("AllGather", ALU.bypass, PAIR_GROUPS, C.X1[i * XCH:(i + 1) * XCH, :], C.G[i], C.B_x1, C.B_G)
    P.barrier()


V_A1, V_AG0, V_AB0, V_A2, V_B2, V_TMP = 0, 1, 2, 3, 4, 5
M_SH1, M_SC1, M_G1, M_SH2, M_SC2, M_G2 = 0, 1, 2, 3, 4, 5


def vcol(t, j, fc):
    return t[:, j * NCH + fc: j * NCH + fc + 1]


def phase_setup(C):
    P, nc = C.P, C.nc
    bc = C.B_const
    P.dma("sp", C.ident_t[:], C.ident[:, :], bc, None, True)
    P.dma("sp", C.flag_t[:], C.flag[:, :], bc, None, True)
    P.op("dve", "memset", C.ones_t[:], 1.0, writes=[bc])
    P.op("dve", "memset", C.onesb_t[:], 1.0, writes=[bc])
    P.op("dve", "tensor_copy", C.identb_t[:], C.ident_t[:], reads=[bc], writes=[bc])
    P.barrier()


def phase_mod(C):
    P, nc, l = C.P, C.nc, C.layer
    HC = 3 * NCH
    with ExitStack() as es:
        c_t = sb(nc, es, "c_t", [128, NCH], F32)
        cs_t = sb(nc, es, "cs_t", [128, NCH], F32)
        ab_t = sb(nc, es, "ab_t", [128, HC], F32)
        mh_t = sb(nc, es, "mh_t", [128, HC], F32)
        W = [sb(nc, es, "adaW%d" % i, [128, NCH, 512], F32) for i in range(2)]
        mps = ps(nc, es, "modps", [128, 512])
        b_c, b_ab, b_ps, b_mh = P.buf("c"), P.buf("ab"), P.pbuf("modps"), P.buf("mh")
        b_W = P.bufs_n("adaW", 2)
        P.dma("sp", c_t[:], C.c_pf[:, :], b_c, None, True)
        P.dma("sp", ab_t[:], C.ada_b[:, :], b_ab, None, True)
        P.op("act", "activation", out=cs_t[:], in_=c_t[:], func=AF.Silu, reads=[b_c], writes=[b_c])
        wv = C.ada_w.rearrange("(kc p) n -> p kc n", p=128)
        for cg in range(HC // 4):
            s = cg % 2
            for part in range(4):
                q = "sp" if part % 2 == 0 else "pool"
                P.dma(q, W[s][:, part * 8:(part + 1) * 8, :], wv[:, part * 8:(part + 1) * 8, cg * 512:(cg + 1) * 512],
                      b_W[s], None, True)
            for cc in range(4):
                col = cg * 4 + cc
                for kc in range(NCH):
                    P.op("pe", "matmul", mps[:, col:col + 1], W[s][:, kc, cc * 128:(cc + 1) * 128], cs_t[:, kc:kc + 1],
                         start=(kc == 0), stop=(kc == NCH - 1), reads=[b_W[s], b_c], writes=[b_ps],
                         sig=(kc == NCH - 1))
        bc = C.B_const
        P.op("dve", "tensor_tensor", mh_t[:], mps[:, 0:HC], ab_t[:], ALU.add, reads=[b_ps, b_ab], writes=[b_mh])
        P.dma("sp", C.Mx[:, :], mh_t[:], b_mh, C.B_Mx, False)
        P.coll("AllGather", ALU.bypass, PAIR_GROUPS, C.Mx[:, :], C.Mg[:, :], C.B_Mx, C.B_Mg)
        P.dma("sp", C.mod_t[:, 0:HC], C.Mg[0:128, :], bc, C.B_Mg, True)
        P.dma("sp", C.mod_t[:, HC:2 * HC], C.Mg[128:256, :], bc, C.B_Mg, True)
        m, v = C.mod_t, C.vec_t
        sl = lambda t, j: t[:, j * NCH:(j + 1) * NCH]
        P.op("dve", "tensor_scalar", sl(v, V_A1), sl(m, M_SC1), 1.0, None, ALU.add, reads=[bc], writes=[bc])
        P.op("dve", "tensor_scalar", sl(v, V_AG0), C.lng_t[:, 0:NCH], ALPHA, None, ALU.mult, reads=[bc], writes=[bc])
        P.op("dve", "tensor_scalar", sl(v, V_AB0), C.lnb_t[:, 0:NCH], ALPHA, None, ALU.mult, reads=[bc], writes=[bc])
        P.op("dve", "tensor_scalar", sl(v, V_TMP), sl(m, M_SC2), 1.0, None, ALU.add, reads=[bc], writes=[bc])
        P.op("dve", "tensor_tensor", sl(v, V_A2), sl(v, V_TMP), C.lng_t[:, 0:NCH], ALU.mult, reads=[bc], writes=[bc])
        P.op("dve", "tensor_tensor", sl(v, V_B2), sl(v, V_TMP), C.lnb_t[:, 0:NCH], ALU.mult, reads=[bc], writes=[bc])
        P.op("dve", "tensor_tensor", sl(v, V_B2), sl(v, V_B2), sl(m, M_SH2), ALU.add, reads=[bc], writes=[bc])
        P.barrier()


def phase_xT(C):
    import os
    XM = int(os.environ.get("XT_MODE", "9"))
    P, nc = C.P, C.nc
    TB = 256
    with ExitStack() as es:
        xin = [sb(nc, es, "xin%d" % i, [128, 2, D], F32) for i in range(2)]
        hT = [sb(nc, es, "hTo%d" % i, [128, NCH, TB], BF16) for i in range(2)]
        xT = [sb(nc, es, "xTo%d" % i, [128, NCH, TB], F32) for i in range(2)]
        pst_ = [ps(nc, es, "xTps%d" % i, [128, 512]) for i in range(4)]
        pst = [p[:, :].rearrange("p (a b) -> p a b", a=4) for p in pst_]
        b_xin, b_hT, b_xT, b_ps = P.bufs_n("xin", 2), P.bufs_n("hTo", 2), P.bufs_n("xTo", 2), P.pbufs_n("xTps", 4)
        hview = C.S_hT.rearrange("c p t -> p c t")
        xview = C.S_xT.rearrange("c p t -> p c t")
        pc = 0
        for tb in range(TE // TB):
            s = tb % 2
            own = tb * TB >= T
            for j in range(2):
                src_ap, src_buf = C.xe_rows(tb * TB + j * 128)
                P.dma("sp", xin[s][:, j, :], src_ap, b_xin[s], src_buf, True)
            for j in range(2):
                for g in range(NCH // 4):
                    pb = pc % 4
                    pc += 1
                    for i in range(4):
                        fc = g * 4 + i
                        if XM < 1:
                            continue
                        P.op("pe", "transpose", pst[pb][:, i, :], xin[s][:, j, fc * 128:(fc + 1) * 128], C.ident_t[:],
                             reads=[b_xin[s], C.B_const], writes=[b_ps[pb]])
                    for i in range(4):
                        fc = g * 4 + i
                        if XM < 2:
                            continue
                        P.op("act", "activation", out=hT[s][:, fc, j * 128:(j + 1) * 128], in_=pst[pb][:, i, :],
                             func=AF.Identity, scale=vcol(C.vec_t, V_A1, fc), bias=vcol(C.mod_t, M_SH1, fc),
                             reads=[b_ps[pb], C.B_const], writes=[b_hT[s]])
                    if own and XM >= 3:
                        P.op("dve", "tensor_scalar", xT[s][:, g * 4:(g + 1) * 4, j * 128:(j + 1) * 128], pst[pb][:, :, :],
                             ALPHA, None, ALU.mult, reads=[b_ps[pb]], writes=[b_xT[s]])
            if XM >= 4:
                P.dma("sp", hview[:, :, tb * TB:(tb + 1) * TB], hT[s][:], b_hT[s], C.B_hT, False)
            if own and XM >= 5:
                t0 = tb * TB - T
                P.dma("sp", xview[:, :, t0:t0 + TB], xT[s][:], b_xT[s], C.B_xT, False)
        P.barrier()


def gemm(C, name, W, K, N, actT, tiles, ntt, colsel, epilogue, wcols=512):
    P, nc = C.P, C.nc
    KC = K // 128
    KG = max(1, KC // 32)
    kcg = min(KC, 32)
    with ExitStack() as es:
        NCG = wcols // 128
        nact = 2 if (2 * KC * 512 * ntt * 2 + 2 * kcg * wcols * 2) <= 150 * 1024 else 1
        act = [sb(nc, es, "%s_act%d" % (name, i), [128, KC, 512 * ntt], BF16) for i in range(nact)]
        Wt = [sb(nc, es, "%s_W%d" % (name, i), [128, kcg, wcols], BF16) for i in range(2)]
        pbank = [ps(nc, es, "%s_ps%d" % (name, i), [128, 512]) for i in range(8)]
        b_act, b_W, b_ps = P.bufs_n(name + "act", nact), P.bufs_n(name + "W", 2), P.pbufs_n(name + "ps", 8)
        st = epilogue.setup(es) if hasattr(epilogue, "setup") else None
        wv = W.rearrange("(kc p) n -> p kc n", p=128)
        av = actT.rearrange("c p t -> p c t")
        wcount = 0
        bcount = 0
        for ti, t0 in enumerate(tiles):
            a = ti % nact
            for part in range(max(1, KC // 16)):
                k0, k1 = part * 16, min(KC, (part + 1) * 16)
                P.dma("sp", act[a][:, k0:k1, :], av[:, k0:k1, t0:t0 + 512 * ntt], b_act[a], epilogue.act_buf, True)
            sel = colsel(t0)
            for cg in range(N // wcols):
                chunks = [cc for cc in range(cg * NCG, cg * NCG + NCG) if cc in sel]
                if not chunks:
                    continue
                if KG > 1:
                    banks = {cc: (bcount % (8 // NCG)) * NCG + (cc - cg * NCG) for cc in chunks}
                    bcount += 1
                for kg in range(KG):
                    ws = wcount % 2
                    wcount += 1
                    nparts = 4 if kcg >= 32 else 1
                    step = kcg // nparts
                    for part in range(nparts):
                        P.dma("pool", Wt[ws][:, part * step:(part + 1) * step, :],
                              wv[:, kg * 32 + part * step: kg * 32 + (part + 1) * step, cg * wcols:(cg + 1) * wcols],
                              b_W[ws], None, True)
                    for cc in chunks:
                        cl = cc - cg * NCG
                        for s in range(ntt):
                            if KG > 1:
                                bk = banks[cc]
                            else:
                                bk = bcount % 8
                                bcount += 1
                            for kc in range(kcg):
                                P.op("pe", "matmul", pbank[bk][:, :], Wt[ws][:, kc, cl * 128:(cl + 1) * 128],
                                     act[a][:, kg * 32 + kc, s * 512:(s + 1) * 512],
                                     start=(kg == 0 and kc == 0), stop=(kg == KG - 1 and kc == kcg - 1),
                                     reads=[b_W[ws], b_act[a]], writes=[b_ps[bk]], sig=(kc == kcg - 1))
                            if kg == KG - 1:
                                epilogue(cc, t0 + s * 512, pbank[bk][:, :], b_ps[bk])
        P.barrier()


class EpiStore:
    def __init__(self, C, act_buf, nslot=4):
        self.C = C
        self.act_buf = act_buf
        self.n = 0
        self.ns = nslot

    def setup(self, es):
        C = self.C
        n = self.ns
        self.st32 = [sb(C.nc, es, "epi32_%d" % i, [128, 512], F32) for i in range(n)]
        self.st16 = [sb(C.nc, es, "epi16_%d" % i, [128, 512], BF16) for i in range(n)]
        self.b32 = C.P.bufs_n("epi32_", n)
        self.b16 = C.P.bufs_n("epi16_", n)
        self.aux = [sb(C.nc, es, "epiaux_%d" % i, [128, 512], F32) for i in range(n)]
        self.baux = C.P.bufs_n("epiaux_", n)


def phase_win(C):
    P, layer = C.P, C.layer
    if layer == 0:
        kv = set(range(48, 80))
        halo = set(range(0, 32))
        f32c = set(range(0, 32))
    else:
        kv = set(range(16, 48))
        halo = set(range(64, 80))
        f32c = set(range(0, 32)) | set(range(48, 80))
    allc = set(range(80))

    def colsel(t0):
        if t0 >= T:
            return allc
        if t0 + 1024 >= T:
            return kv | halo
        return kv

    epi = EpiStore(C, C.B_hT)

    def ep(cc, t0, pap, pbuf):
        i = epi.n % epi.ns
        epi.n += 1
        if cc in f32c:
            P.op("act", "activation", out=epi.st32[i][:], in_=pap, func=AF.Copy, reads=[pbuf], writes=[epi.b32[i]])
            P.dma("sp", C.S_u32[cc, :, t0:t0 + 512], epi.st32[i][:], epi.b32[i], C.B_u, False)
        else:
            P.op("act", "activation", out=epi.st16[i][:], in_=pap, func=AF.Copy, reads=[pbuf], writes=[epi.b16[i]])
            P.dma("sp", C.S_u16[cc, :, t0:t0 + 512], epi.st16[i][:], epi.b16[i], C.B_u, False)

    ep.setup = epi.setup
    ep.act_buf = C.B_hT
    gemm(C, "win", C.w_in, D, 5 * MIX, C.S_hT, [0, 1024, 2048, 3072], 2, colsel, ep)


def phase_wout(C, ):
    res_gemm(C, "wout", C.w_out, D, C.S_mix, C.B_mix, M_G1, ntt=2)


def res_gemm(C, name, W, K, actT, act_buf, gidx, ntt, wcols=512, nslot=4, src=None, dst=None):
    P = C.P
    epi = EpiStore(C, act_buf, nslot)
    src_ap, src_buf = src if src is not None else (C.S_xT, C.B_xT)
    dst_ap, dst_buf = dst if dst is not None else (C.S_z, C.B_z)

    def ep(cc, t0, pap, pbuf):
        i = epi.n % epi.ns
        epi.n += 1
        P.dma("sp", epi.aux[i][:], src_ap[cc, :, t0:t0 + 512], epi.baux[i], src_buf, True)
        P.op("dve", "scalar_tensor_tensor", epi.st32[i][:], pap, vcol(C.mod_t, gidx, cc), epi.aux[i][:],
             ALU.mult, ALU.add, reads=[pbuf, epi.baux[i], C.B_const], writes=[epi.b32[i]])
        P.dma("sp", dst_ap[cc, :, t0:t0 + 512], epi.st32[i][:], epi.b32[i], dst_buf, False)

    ep.setup = epi.setup
    ep.act_buf = act_buf
    allc = set(range(NCH))
    tiles = [i * 512 * ntt for i in range(T // (512 * ntt))]
    gemm(C, name, W, K, D, actT, tiles, ntt, lambda t0: allc, ep, wcols=wcols)


def phase_mlp1(C):
    P = C.P
    epi = EpiStore(C, C.B_h2)

    def ep(cc, t0, pap, pbuf):
        i = epi.n % epi.ns
        epi.n += 1
        P.op("act", "activation", out=epi.st32[i][:], in_=pap, func=AF.Relu, reads=[pbuf], writes=[epi.b32[i]])
        P.op("dve", "tensor_tensor", epi.st16[i][:], epi.st32[i][:], epi.st32[i][:], ALU.mult,
             reads=[epi.b32[i]], writes=[epi.b16[i]])
        P.dma("sp", C.S_hid[cc, :, t0:t0 + 512], epi.st16[i][:], epi.b16[i], C.B_hid, False)

    ep.setup = epi.setup
    ep.act_buf = C.B_h2
    allc = set(range(128))
    gemm(C, "mlp1", C.w1, D, 4 * D, C.S_h2, [0, 1024], 2, lambda t0: allc, ep)


def phase_mlp2(C):
    zs = [(C.S_xT, C.B_xT), (C.S_z, C.B_z), (C.S_z2, C.B_z2), (C.S_z, C.B_z), (C.S_z2, C.B_z2)]
    for kg in range(4):
        res_gemm(C, "mlp2_%d" % kg, C.w2[kg * D:(kg + 1) * D, :], D, C.S_hid[kg * NCH:(kg + 1) * NCH], C.B_hid,
                 M_G2, ntt=2, src=zs[kg], dst=zs[kg + 1])


def ln_stats(C, es, name, zt, nch, nfeat, b_z, width=512):
    P, nc = C.P, C.nc
    sq = [sb(nc, es, name + "_sq%d" % i, [128, width], F32) for i in range(2)]
    b_sq = P.bufs_n(name + "sq", 2)
    p_sum = ps(nc, es, name + "_psum", [128, width])
    p_sq = ps(nc, es, name + "_psq", [128, width])
    b_ps = P.pbuf(name + "pss")
    mean = sb(nc, es, name + "_mean", [128, width], F32)
    msq = sb(nc, es, name + "_msq", [128, width], F32)
    rstd = sb(nc, es, name + "_rstd", [128, width], F32)
    b_st = P.buf(name + "stat")
    return dict(sq=sq, b_sq=b_sq, p_sum=p_sum, p_sq=p_sq, b_ps=b_ps, mean=mean, msq=msq, rstd=rstd, b_st=b_st,
                nch=nch, nfeat=nfeat)


def ln_stats_run(C, S, zt, b_z):
    P = C.P
    nch = S["nch"]
    for fc in range(nch):
        i = fc % 2
        P.op("act", "activation", out=S["sq"][i][:], in_=zt[:, fc, :], func=AF.Square, reads=[b_z], writes=[S["b_sq"][i]])
        P.op("pe", "matmul", S["p_sum"][:], C.ones_t[:], zt[:, fc, :], start=(fc == 0), stop=(fc == nch - 1),
             reads=[b_z, C.B_const], writes=[S["b_ps"]])
        P.op("pe", "matmul", S["p_sq"][:], C.ones_t[:], S["sq"][i][:], start=(fc == 0), stop=(fc == nch - 1),
             reads=[S["b_sq"][i], C.B_const], writes=[S["b_ps"]])
    inv = 1.0 / S["nfeat"]
    b_st = S["b_st"]
    P.op("dve", "tensor_scalar", S["mean"][:], S["p_sum"][:], inv, None, ALU.mult, reads=[S["b_ps"]], writes=[b_st])
    P.op("dve", "tensor_scalar", S["msq"][:], S["p_sq"][:], inv, None, ALU.mult, reads=[S["b_ps"]], writes=[b_st])
    P.op("dve", "tensor_tensor", S["rstd"][:], S["mean"][:], S["mean"][:], ALU.mult, reads=[b_st], writes=[b_st])
    P.op("dve", "tensor_tensor", S["msq"][:], S["msq"][:], S["rstd"][:], ALU.subtract, reads=[b_st], writes=[b_st])
    P.op("dve", "tensor_scalar", S["msq"][:], S["msq"][:], EPS, None, ALU.add, reads=[b_st], writes=[b_st])
    P.op("act", "activation", out=S["msq"][:], in_=S["msq"][:], func=AF.Sqrt, reads=[b_st], writes=[b_st])
    P.op("dve", "reciprocal", S["rstd"][:], S["msq"][:], reads=[b_st], writes=[b_st])


def phase_ln(C, which):
    P, nc = C.P, C.nc
    with ExitStack() as es:
        zts = [sb(nc, es, "ln_z%d" % i, [128, NCH, 512], F32) for i in range(2)]
        b_zs = P.bufs_n("ln_z", 2)
        S = ln_stats(C, es, "ln", zts[0], NCH, D, b_zs[0])
        if which == 0:
            o32 = [sb(nc, es, "ln_o32_%d" % i, [128, 512], F32) for i in range(3)]
            o16 = [sb(nc, es, "ln_o16_%d" % i, [128, 512], BF16) for i in range(3)]
            b32, b16 = P.bufs_n("ln_o32_", 3), P.bufs_n("ln_o16_", 3)
        else:
            rows = [sb(nc, es, "ln_rows%d" % i, [128, D], F32) for i in range(2)]
            b_rows = P.bufs_n("ln_rows", 2)
            pst = [ps(nc, es, "ln_tps%d" % i, [128, 512]) for i in range(4)]
            b_pst = P.pbufs_n("ln_tps", 4)
        zsrc, zbuf = (C.S_z, C.B_z) if which == 0 else (C.S_z2, C.B_z2)
        zv = zsrc.rearrange("c p t -> p c t")
        pc = 0
        rc = 0
        for tt in range(T // 512):
            t0 = tt * 512
            zt, b_z = zts[tt % 2], b_zs[tt % 2]
            for part in range(4):
                P.dma("sp", zt[:, part * 8:(part + 1) * 8, :], zv[:, part * 8:(part + 1) * 8, t0:t0 + 512], b_z, zbuf, True)
            ln_stats_run(C, S, zt, b_z)
            for fc in range(NCH):
                P.op("dve", "tensor_tensor", zt[:, fc, :], zt[:, fc, :], S["mean"][:], ALU.subtract,
                     reads=[b_z, S["b_st"]], writes=[b_z])
                P.op("dve", "tensor_tensor", zt[:, fc, :], zt[:, fc, :], S["rstd"][:], ALU.mult,
                     reads=[b_z, S["b_st"]], writes=[b_z])
                if which == 0:
                    i = (tt * NCH + fc) % 3
                    P.op("act", "activation", out=o32[i][:], in_=zt[:, fc, :], func=AF.Identity,
                         scale=vcol(C.vec_t, V_AG0, fc), bias=vcol(C.vec_t, V_AB0, fc),
                         reads=[b_z, C.B_const], writes=[b32[i]])
                    P.dma("sp", C.S_xT[fc, :, t0:t0 + 512], o32[i][:], b32[i], C.B_xT, False)
                    P.op("act", "activation", out=o16[i][:], in_=zt[:, fc, :], func=AF.Identity,
                         scale=vcol(C.vec_t, V_A2, fc), bias=vcol(C.vec_t, V_B2, fc),
                         reads=[b_z, C.B_const], writes=[b16[i]])
                    P.dma("sp", C.S_h2[fc, :, t0:t0 + 512], o16[i][:], b16[i], C.B_h2, False)
                else:
                    P.op("act", "activation", out=zt[:, fc, :], in_=zt[:, fc, :], func=AF.Identity,
                         scale=C.lng_t[:, NCH + fc:NCH + fc + 1], bias=C.lnb_t[:, NCH + fc:NCH + fc + 1],
                         reads=[b_z, C.B_const], writes=[b_z])
            if which == 1:
                for blk in range(4):
                    r = rc % 2
                    rc += 1
                    for g in range(NCH // 4):
                        pb = pc % 4
                        pc += 1
                        for i in range(4):
                            fc = g * 4 + i
                            P.op("pe", "transpose", pst[pb][:, i * 128:(i + 1) * 128], zt[:, fc, blk * 128:(blk + 1) * 128], C.ident_t[:],
                                 reads=[b_z, C.B_const], writes=[b_pst[pb]])
                        eng = "act" if g % 2 == 0 else "dve"
                        if eng == "act":
                            P.op("act", "activation", out=rows[r][:, g * 512:(g + 1) * 512], in_=pst[pb][:, :],
                                 func=AF.Copy, reads=[b_pst[pb]], writes=[b_rows[r]])
                        else:
                            P.op("dve", "tensor_copy", rows[r][:, g * 512:(g + 1) * 512], pst[pb][:, :],
                                 reads=[b_pst[pb]], writes=[b_rows[r]])
                    dst_ap, dst_buf = C.out_rows(t0 + blk * 128)
                    P.dma("sp", dst_ap, rows[r][:], b_rows[r], dst_buf, False)
        P.barrier()


def phase_conv(C):
    P, nc = C.P, C.nc
    W = HALO + T
    with ExitStack() as es:
        cw = sb(nc, es, "cv_w", [128, 16 * CONV_K], F32)
        cb = sb(nc, es, "cv_b", [128, 16], F32)
        b_cw = P.buf("cv_w")
        P.dma("sp", cw[:], C.conv_w[:, :], b_cw, None, True)
        P.dma("sp", cb[:], C.conv_b[:, :], b_cw, None, True)
        av = [sb(nc, es, "cv_av%d" % i, [128, W], F32) for i in range(2)]
        ag = [sb(nc, es, "cv_ag%d" % i, [128, W], F32) for i in range(2)]
        acc = [sb(nc, es, "cv_acc%d" % i, [128, T], F32) for i in range(2)]
        b_av, b_ag, b_acc = P.bufs_n("cv_av", 2), P.bufs_n("cv_ag", 2), P.bufs_n("cv_acc", 2)
        for cc in range(16):
            s = cc % 2
            eng = "dve"
            P.dma("sp", av[s][:], C.S_u32[cc, :, T - HALO:TE], b_av[s], C.B_u, True)
            P.dma("sp", ag[s][:], C.S_u32[16 + cc, :, T - HALO:TE], b_ag[s], C.B_u, True)
            P.op("act", "activation", out=ag[s][:], in_=ag[s][:], func=AF.Sigmoid, reads=[b_ag[s]], writes=[b_ag[s]])
            P.op(eng, "tensor_tensor", av[s][:], av[s][:], ag[s][:], ALU.mult, reads=[b_av[s], b_ag[s]], writes=[b_av[s]])
            P.op(eng, "tensor_scalar", av[s][:, 0:HALO], av[s][:, 0:HALO], C.flag_t[:, 0:1], None, ALU.mult,
                 reads=[b_av[s], C.B_const], writes=[b_av[s]])
            o0 = HALO - (CONV_K - 1)
            P.op(eng, "tensor_scalar", acc[s][:], av[s][:, o0:o0 + T], cw[:, cc * CONV_K:cc * CONV_K + 1], cb[:, cc:cc + 1],
                 ALU.mult, ALU.add, reads=[b_av[s], b_cw], writes=[b_acc[s]])
            for j in range(1, CONV_K):
                P.op(eng, "scalar_tensor_tensor", acc[s][:], av[s][:, o0 + j:o0 + j + T],
                     cw[:, cc * CONV_K + j:cc * CONV_K + j + 1], acc[s][:], ALU.mult, ALU.add,
                     reads=[b_av[s], b_cw, b_acc[s]], writes=[b_acc[s]])
            P.dma("sp", C.S_cv[cc, :, :], acc[s][:], b_acc[s], C.B_cv, False)
        P.barrier()
    with ExitStack() as es:
        g = sb(nc, es, "cl_g", [128, 16], F32)
        bb = sb(nc, es, "cl_b", [128, 16], F32)
        b_g = P.buf("cl_g")
        P.dma("sp", g[:], C.conv_g[:, :], b_g, None, True)
        P.dma("sp", bb[:], C.conv_bb[:, :], b_g, None, True)
        zt = sb(nc, es, "cl_z", [128, 16, 512], F32)
        b_z = P.buf("cl_z")
        S = ln_stats(C, es, "cl", zt, 16, MIX, b_z)
        o16 = [sb(nc, es, "cl_o%d" % i, [128, 512], BF16) for i in range(3)]
        b16 = P.bufs_n("cl_o", 3)
        zv = C.S_cv.rearrange("c p t -> p c t")
        for tt in range(T // 512):
            t0 = tt * 512
            for part in range(2):
                P.dma("sp", zt[:, part * 8:(part + 1) * 8, :], zv[:, part * 8:(part + 1) * 8, t0:t0 + 512], b_z, C.B_cv, True)
            ln_stats_run(C, S, zt, b_z)
            for cc in range(16):
                P.op("dve", "tensor_tensor", zt[:, cc, :], zt[:, cc, :], S["mean"][:], ALU.subtract,
                     reads=[b_z, S["b_st"]], writes=[b_z])
                P.op("dve", "tensor_tensor", zt[:, cc, :], zt[:, cc, :], S["rstd"][:], ALU.mult,
                     reads=[b_z, S["b_st"]], writes=[b_z])
                i = (tt * 16 + cc) % 3
                P.op("act", "activation", out=o16[i][:], in_=zt[:, cc, :], func=AF.Silu,
                     scale=g[:, cc:cc + 1], bias=bb[:, cc:cc + 1], reads=[b_z, b_g], writes=[b16[i]])
                P.dma("sp", C.S_mix[cc, :, t0:t0 + 512], o16[i][:], b16[i], C.B_mix, False)
        P.barrier()


def phase_attn(C):
    P, nc = C.P, C.nc
    scale = 128.0 ** -0.5
    DILS = (1, 4, 16)
    with ExitStack() as es:
        qT = sb(nc, es, "at_q", [128, T], BF16)
        kT = sb(nc, es, "at_k", [128, TE], BF16)
        vT = sb(nc, es, "at_v", [128, TE], BF16)
        bm = sb(nc, es, "at_bm", [128, 3, 256], F32)
        mF = sb(nc, es, "at_mF", [128, 3, 256], F32)
        Vt = sb(nc, es, "at_Vt", [128, 3, 32, 128], BF16)
        acc = sb(nc, es, "at_acc", [128, 2, T], F32)
        outb = sb(nc, es, "at_out", [128, T], BF16)
        pt = [sb(nc, es, "at_pt%d" % i, [128, 256], F32) for i in range(3)]
        ptm = [sb(nc, es, "at_ptm%d" % i, [128, 256], BF16) for i in range(3)]
        b_q, b_k, b_v, b_bm, b_mF, b_Vt, b_acc, b_out = (P.buf(n) for n in
                                                       ("at_q", "at_k", "at_v", "at_bm", "at_mF", "at_Vt", "at_acc", "at_out"))
        b_pt, b_ptm = P.bufs_n("at_pt", 3), P.bufs_n("at_ptm", 3)
        p_s = [ps(nc, es, "at_ps%d" % i, [128, 512])[:, 0:256].rearrange("p (a b) -> p a b", a=2) for i in range(3)]
        p_o = [ps(nc, es, "at_po%d" % i, [128, 512])[:, 0:256].rearrange("p (a b) -> p a b", a=2) for i in range(3)]
        p_v = [ps(nc, es, "at_pv%d" % i, [128, 512])[:, :].rearrange("p (a b) -> p a b", a=4) for i in range(2)]
        b_ps, b_po, b_pv = P.pbufs_n("at_ps", 3), P.pbufs_n("at_po", 3), P.pbufs_n("at_pv", 2)
        it = 0
        vc = 0
        for h in range(16):
            P.dma("sp", qT[:], C.S_u16[32 + h, :, T:TE], b_q, C.B_u, True)
            P.dma("sp", kT[:], C.S_u16[48 + h, :, :], b_k, C.B_u, True)
            P.dma("sp", vT[:], C.S_u16[64 + h, :, :], b_v, C.B_u, True)
            P.dma("sp", bm[:], C.biasmat[h].rearrange("p (d k) -> p d k", d=3), b_bm, None, True)
            P.op("act", "activation", out=bm[:], in_=bm[:], func=AF.Exp, reads=[b_bm], writes=[b_bm])
            P.op("dve", "tensor_scalar", mF[:, :, 0:128], bm[:, :, 0:128], C.flag_t[:, 0:1], None, ALU.mult,
                 reads=[b_bm, C.B_const], writes=[b_mF])
            P.op("dve", "tensor_copy", mF[:, :, 128:256], bm[:, :, 128:256], reads=[b_bm], writes=[b_mF])
            P.op("pool", "memset", acc[:], 0.0, writes=[b_acc])
            for di, dl in enumerate(DILS):
                for g in range(8):
                    pv = vc % 2
                    vc += 1
                    for i in range(4):
                        blk = g * 4 + i
                        n, r = blk // dl, blk % dl
                        base = 128 * dl * n + r
                        P.op("pe", "matmul", p_v[pv][:, i, :], vT[:, base:base + 127 * dl + 1:dl], C.identb_t[:],
                             start=True, stop=True, reads=[b_v, C.B_const], writes=[b_pv[pv]])
                    P.op("act", "activation", out=Vt[:, di, g * 4:(g + 1) * 4, :], in_=p_v[pv][:, :, :], func=AF.Copy,
                         reads=[b_pv[pv]], writes=[b_Vt])
            for di, dl in enumerate(DILS):
                nsb = 32 // dl
                for n in range(nsb // 2, nsb):
                    for r in range(dl):
                        blk_c = n * dl + r
                        blk_p = (n - 1) * dl + r
                        base_c = 128 * dl * n + r
                        base_p = 128 * dl * (n - 1) + r
                        qs = slice(base_c - T, base_c - T + 127 * dl + 1, dl)
                        i2 = it % 3
                        it += 1
                        P.op("pe", "matmul", p_s[i2][:, 0, :], kT[:, base_p:base_p + 127 * dl + 1:dl], qT[:, qs],
                             start=True, stop=True, reads=[b_k, b_q], writes=[b_ps[i2]])
                        P.op("pe", "matmul", p_s[i2][:, 1, :], kT[:, base_c:base_c + 127 * dl + 1:dl], qT[:, qs],
                             start=True, stop=True, reads=[b_k, b_q], writes=[b_ps[i2]])
                        P.op("act", "activation", out=pt[i2][:], in_=p_s[i2][:, :, :], func=AF.Exp, scale=scale,
                             reads=[b_ps[i2]], writes=[b_pt[i2]])
                        first = (n == nsb // 2)
                        msk = mF if first else bm
                        P.op("dve", "tensor_tensor", ptm[i2][:], pt[i2][:], msk[:, di, :], ALU.mult,
                             reads=[b_pt[i2], b_mF if first else b_bm], writes=[b_ptm[i2]])
                        P.op("pe", "matmul", p_o[i2][:, 0, :], Vt[:, di, blk_p, :], ptm[i2][:, 0:128],
                             start=True, stop=False, reads=[b_Vt, b_ptm[i2]], writes=[b_po[i2]])
                        P.op("pe", "matmul", p_o[i2][:, 0, :], Vt[:, di, blk_c, :], ptm[i2][:, 128:256],
                             start=False, stop=True, reads=[b_Vt, b_ptm[i2]], writes=[b_po[i2]])
                        P.op("pe", "matmul", p_o[i2][:, 1, :], C.onesb_t[:], ptm[i2][:, 0:128],
                             start=True, stop=False, reads=[C.B_const, b_ptm[i2]], writes=[b_po[i2]])
                        P.op("pe", "matmul", p_o[i2][:, 1, :], C.onesb_t[:], ptm[i2][:, 128:256],
                             start=False, stop=True, reads=[C.B_const, b_ptm[i2]], writes=[b_po[i2]])
                        P.op("dve", "tensor_tensor", acc[:, :, qs], acc[:, :, qs], p_o[i2][:, :, :], ALU.add,
                             reads=[b_acc, b_po[i2]], writes=[b_acc])
            P.op("dve", "reciprocal", acc[:, 1, :], acc[:, 1, :], reads=[b_acc], writes=[b_acc])
            P.op("dve", "tensor_tensor", outb[:], acc[:, 0, :], acc[:, 1, :], ALU.mult, reads=[b_acc], writes=[b_out])
            P.dma("sp", C.S_mix[16 + h, :, :], outb[:], b_out, C.B_mix, False)
        P.barrier()


def phase_hgrn(C):
    P, nc = C.P, C.nc
    NCK = TE // 64
    with ExitStack() as es:
        lbl = sb(nc, es, "hg_lbl", [128, 32], F32)
        lb = sb(nc, es, "hg_lb", [128, 16], F32)
        omlb = sb(nc, es, "hg_omlb", [128, 16], F32)
        ng = sb(nc, es, "hg_ng", [128, 16], F32)
        cm = sb(nc, es, "hg_cm", [64, 64], F32)
        b_c = P.buf("hg_const")
        P.dma("sp", lbl[:], C.lbl[:, :], b_c, None, True)
        P.dma("sp", ng[:], C.hng[:, :], b_c, None, True)
        P.dma("sp", cm[:], C.cmask[:, :], b_c, None, True)
        P.op("dve", "tensor_tensor", lb[:], lbl[:, 16:32], lbl[:, 0:16], ALU.subtract, reads=[b_c], writes=[b_c])
        P.op("act", "activation", out=lb[:], in_=lb[:], func=AF.Sigmoid, reads=[b_c], writes=[b_c])
        P.op("dve", "tensor_scalar", omlb[:], lb[:], -1.0, 1.0, ALU.mult, ALU.add, reads=[b_c], writes=[b_c])

        fT = sb(nc, es, "hg_f", [128, TE], F32)
        kk = sb(nc, es, "hg_kk", [128, TE], F32)
        bA = sb(nc, es, "hg_bA", [128, NCK, 64], F32)
        bB = sb(nc, es, "hg_bB", [128, NCK, 64], F32)
        ebl = sb(nc, es, "hg_ebl", [128, NCK], F32)
        qT = sb(nc, es, "hg_q", [128, T], F32)
        qe = sb(nc, es, "hg_qe", [128, T], BF16)
        ke = sb(nc, es, "hg_ke", [128, T], BF16)
        ke2 = sb(nc, es, "hg_ke2", [128, TE], BF16)
        iT = sb(nc, es, "hg_i", [128, TE], BF16)
        gT = sb(nc, es, "hg_g", [128, T], F32)
        tmp = sb(nc, es, "hg_tmp", [128, TE], F32)
        ke2tm = sb(nc, es, "hg_ke2tm", [64, NCK, 128], BF16)
        vtm = sb(nc, es, "hg_vtm", [64, NCK, 128], BF16)
        St = sb(nc, es, "hg_S", [128, 128], F32)
        Sb = sb(nc, es, "hg_Sb", [128, 128], BF16)
        oT = sb(nc, es, "hg_o", [128, T], F32)
        osq = sb(nc, es, "hg_osq", [128, T], F32)
        outb = sb(nc, es, "hg_out", [128, T], BF16)
        am = [sb(nc, es, "hg_am%d" % i, [64, 64], BF16) for i in range(2)]
        names = ("f", "kk", "bA", "bB", "ebl", "q", "qe", "ke", "ke2", "i", "g", "tmp", "ke2tm", "vtm", "S", "Sb", "o",
                 "osq", "out")
        B = {n: P.buf("hg_" + n) for n in names}
        b_am = P.bufs_n("hg_am", 2)
        b_tmpq = P.bufs_n("hg_tmpq", 4)
        p_t = [ps(nc, es, "hg_pt%d" % i, [128, 512])[0:64, :].rearrange("p (a b) -> p a b", a=4) for i in range(2)]
        pk = [ps(nc, es, "hg_pk%d" % i, [128, 512]) for i in range(2)]
        p_S = [pk[i][:, 0:128] for i in range(2)]
        p_a = [pk[i][0:64, 128:192] for i in range(2)]
        p_o = [pk[i][:, 192:256] for i in range(2)]
        b_pt = P.pbufs_n("hg_pt", 2)
        b_pS = P.pbufs_n("hg_pk", 2)
        b_pa = b_pS
        b_po = b_pS
        p_n = ps(nc, es, "hg_pn", [128, 512])
        b_pn = P.pbuf("hg_pn")
        tcnt = 0
        for h in range(16):
            P.dma("sp", fT[:], C.S_u32[16 + h, :, :], B["f"], C.B_u, True)
            P.dma("sp", qT[:], C.S_u32[h, :, T:TE], B["q"], C.B_u, True)
            P.dma("sp", iT[:], C.S_u16[32 + h, :, :], B["i"], C.B_u, True)
            P.dma("sp", gT[:], C.S_u32[48 + h, :, T:TE], B["g"], C.B_u, True)
            P.op("act", "activation", out=fT[:], in_=fT[:], func=AF.Sigmoid, reads=[B["f"]], writes=[B["f"]])
            P.op("dve", "tensor_scalar", fT[:], fT[:], omlb[:, h:h + 1], lb[:, h:h + 1], ALU.mult, ALU.add,
                 reads=[B["f"], b_c], writes=[B["f"]])
            P.op("pool", "tensor_scalar", kk[:], fT[:], -1.0, 1.0, ALU.mult, ALU.add, reads=[B["f"]], writes=[B["kk"]])
            P.op("act", "activation", out=bA[:].rearrange("p c t -> p (c t)"), in_=fT[:], func=AF.Ln,
                 reads=[B["f"], B["kk"]], writes=[B["bA"]])
            src, dst, bs, bd = bA, bB, B["bA"], B["bB"]
            for sft in (1, 2, 4, 8, 16, 32):
                P.op("pool", "tensor_copy", dst[:, :, 0:sft], src[:, :, 0:sft], reads=[bs], writes=[bd])
                P.op("dve", "tensor_tensor", dst[:, :, sft:64], src[:, :, sft:64], src[:, :, 0:64 - sft], ALU.add,
                     reads=[bs], writes=[bd])
                src, dst, bs, bd = dst, src, bd, bs
            bcum, b_b = src, bs
            oth, b_oth = dst, bd
            P.op("act", "activation", out=ebl[:], in_=bcum[:, :, 63], func=AF.Exp, reads=[b_b], writes=[B["ebl"]])
            for c in range(NCK):
                P.op("act", "activation", out=tmp[:, c * 64:(c + 1) * 64], in_=bcum[:, c, :], func=AF.Exp, scale=-1.0,
                     bias=bcum[:, c, 63:64], reads=[b_b], writes=[b_tmpq[c % 4]])
            P.op("dve", "tensor_tensor", ke2[:], tmp[:], kk[:], ALU.mult, reads=b_tmpq + [B["kk"]], writes=[B["ke2"]])
            bown = bcum[:, NCK // 2:NCK, :].rearrange("p c t -> p (c t)")
            P.op("act", "activation", out=qT[:], in_=qT[:], func=AF.Silu, reads=[B["q"]], writes=[B["q"]])
            P.op("act", "activation", out=oth[:, 0:NCK // 2, :].rearrange("p c t -> p (c t)"), in_=bown, func=AF.Exp,
                 reads=[b_b], writes=[b_oth])
            P.op("dve", "tensor_tensor", qe[:], qT[:], oth[:, 0:NCK // 2, :].rearrange("p c t -> p (c t)"), ALU.mult,
                 reads=[B["q"], b_oth], writes=[B["qe"]])
            P.op("act", "activation", out=oth[:, NCK // 2:NCK, :].rearrange("p c t -> p (c t)"), in_=bown, func=AF.Exp,
                 scale=-1.0, reads=[b_b], writes=[b_oth])
            P.op("dve", "tensor_tensor", ke[:], kk[:, T:TE], oth[:, NCK // 2:NCK, :].rearrange("p c t -> p (c t)"),
                 ALU.mult, reads=[B["kk"], b_oth], writes=[B["ke"]])
            P.op("act", "activation", out=gT[:], in_=gT[:], func=AF.Silu, reads=[B["g"]], writes=[B["g"]])
            for srcT, dstm, bsrc, bdst in ((ke2, ke2tm, B["ke2"], B["ke2tm"]), (iT, vtm, B["i"], B["vtm"])):
                for g4 in range(NCK // 4):
                    pi = tcnt % 2
                    tcnt += 1
                    for i in range(4):
                        c = g4 * 4 + i
                        P.op("pe", "matmul", p_t[pi][:, i, :], srcT[:, c * 64:(c + 1) * 64], C.identb_t[:],
                             start=True, stop=True, reads=[bsrc, C.B_const], writes=[b_pt[pi]])
                    P.op("act", "activation", out=dstm[:, g4 * 4:(g4 + 1) * 4, :], in_=p_t[pi][:, :, :], func=AF.Copy,
                         reads=[b_pt[pi]], writes=[bdst])
            P.op("pool", "memset", St[:], 0.0, writes=[B["S"]])
            P.op("pool", "memset", Sb[:], 0.0, writes=[B["Sb"]])
            for c in range(NCK):
                i2 = c % 2
                if c >= NCK // 2:
                    co = c - NCK // 2
                    cs_ = slice(co * 64, (co + 1) * 64)
                    P.op("pe", "matmul", p_a[i2][:, :], ke[:, cs_], qe[:, cs_], start=True, stop=True,
                         reads=[B["ke"], B["qe"]], writes=[b_pa[i2]])
                    P.op("dve", "tensor_tensor", am[i2][:], p_a[i2][:, :], cm[:], ALU.mult,
                         reads=[b_pa[i2], b_c], writes=[b_am[i2]])
                    P.op("pe", "matmul", p_o[i2][:, :], Sb[:], qe[:, cs_], start=True, stop=False,
                         reads=[B["Sb"], B["qe"]], writes=[b_po[i2]])
                    P.op("pe", "matmul", p_o[i2][:, :], vtm[:, c, :], am[i2][:], start=False, stop=True,
                         reads=[B["vtm"], b_am[i2]], writes=[b_po[i2]])
                    P.op("act", "activation", out=oT[:, cs_], in_=p_o[i2][:, :], func=AF.Copy,
                         reads=[b_po[i2]], writes=[B["o"]])
                if c == NCK - 1:
                    break
                P.op("pe", "matmul", p_S[i2][:, :], ke2tm[:, c, :], vtm[:, c, :], start=True, stop=True,
                     reads=[B["ke2tm"], B["vtm"]], writes=[b_pS[i2]])
                P.op("dve", "scalar_tensor_tensor", St[:], St[:], ebl[:, c:c + 1], p_S[i2][:, :], ALU.mult, ALU.add,
                     reads=[B["S"], B["ebl"], b_pS[i2]], writes=[B["S"]])
                if c == NCK // 2 - 1:
                    P.op("dve", "tensor_scalar", St[:], St[:], C.flag_t[:, 0:1], None, ALU.mult,
                         reads=[B["S"], C.B_const], writes=[B["S"]])
                if c >= NCK // 2 - 1:
                    P.op("pool", "tensor_copy", Sb[:], St[:], reads=[B["S"]], writes=[B["Sb"]])
            P.op("act", "activation", out=osq[:], in_=oT[:], func=AF.Square, reads=[B["o"]], writes=[B["osq"]])
            for tt in range(T // 512):
                ts_ = slice(tt * 512, (tt + 1) * 512)
                P.op("pe", "matmul", p_n[:, :], C.ones_t[:], osq[:, ts_], start=True, stop=True,
                     reads=[B["osq"], C.B_const], writes=[b_pn])
                P.op("dve", "tensor_scalar", tmp[:, ts_], p_n[:, :], 1.0 / 128.0, EPS, ALU.mult, ALU.add,
                     reads=[b_pn], writes=[B["tmp"]] + b_tmpq)
            P.op("act", "activation", out=tmp[:, 0:T], in_=tmp[:, 0:T], func=AF.Sqrt, reads=[B["tmp"]],
                 writes=[B["tmp"]] + b_tmpq)
            P.op("dve", "reciprocal", tmp[:, 0:T], tmp[:, 0:T], reads=[B["tmp"]], writes=[B["tmp"]] + b_tmpq)
            P.op("dve", "tensor_tensor", oT[:], oT[:], tmp[:, 0:T], ALU.mult, reads=[B["o"], B["tmp"]] + b_tmpq,
                 writes=[B["o"]])
            P.op("dve", "scalar_tensor_tensor", outb[:], oT[:], ng[:, h:h + 1], gT[:], ALU.mult, ALU.mult,
                 reads=[B["o"], B["g"], b_c], writes=[B["out"]])
            P.dma("sp", C.S_mix[h, :, :], outb[:], B["out"], C.B_mix, False)
        P.barrier()


def phase_pool(C):
    P, nc = C.P, C.nc
    W = HALO + T
    with ExitStack() as es:
        ic = sb(nc, es, "pl_ic", [128, 4, T], F32)
        b_ic = P.buf("pl_ic")
        P.dma("sp", ic[:], C.invcnt.rearrange("p (g t) -> p g t", g=4), b_ic, None, True)
        p0 = [sb(nc, es, "pl_p%d" % i, [128, W], F32) for i in range(2)]
        sA = [sb(nc, es, "pl_a%d" % i, [128, W], F32) for i in range(2)]
        sB = [sb(nc, es, "pl_b%d" % i, [128, W], F32) for i in range(2)]
        o16 = [sb(nc, es, "pl_o%d" % i, [128, T], BF16) for i in range(2)]
        b_p, b_a, b_b, b_o = P.bufs_n("pl_p", 2), P.bufs_n("pl_a", 2), P.bufs_n("pl_b", 2), P.bufs_n("pl_o", 2)
        for cc in range(16):
            s = cc % 2
            gi = cc // 4
            eng = "dve"
            P.dma("sp", p0[s][:], C.S_u32[64 + cc, :, T - HALO:TE], b_p[s], C.B_u, True)
            P.op(eng, "tensor_scalar", p0[s][:, 0:HALO], p0[s][:, 0:HALO], C.flag_t[:, 0:1], None, ALU.mult,
                 reads=[b_p[s], C.B_const], writes=[b_p[s]])
            src, bsrc = p0[s], b_p[s]
            bufs = [(sA[s], b_a[s]), (sB[s], b_b[s])]
            sh = 1
            for k in range(gi + 1):
                dst, bdst = bufs[k % 2]
                P.op(eng, "tensor_tensor", dst[:, 16:W], src[:, 16:W], src[:, 16 - sh:W - sh], ALU.add,
                     reads=[bsrc], writes=[bdst])
                src, bsrc = dst, bdst
                sh *= 2
            P.op(eng, "tensor_tensor", src[:, HALO:W], src[:, HALO:W], ic[:, gi, :], ALU.mult,
                 reads=[bsrc, b_ic], writes=[bsrc])
            P.op(eng, "tensor_tensor", o16[s][:], src[:, HALO:W], p0[s][:, HALO:W], ALU.subtract,
                 reads=[bsrc, b_p[s]], writes=[b_o[s]])
            P.dma("sp", C.S_pl[cc, :, :], o16[s][:], b_o[s], C.B_pl, False)
        P.barrier()
    for g in range(4):
        epi = EpiStore(C, C.B_pl)
        with ExitStack() as es2:
            psc = sb(nc, es2, "pl_sc%d" % g, [128, 16], F32)
            b_sc = P.buf("pl_sc")
            P.dma("sp", psc[:], C.pool_s[:, :], b_sc, None, True)

            def ep(cc, t0, pap, pbuf, g=g, epi=epi, psc=psc, b_sc=b_sc):
                i = epi.n % epi.ns
                epi.n += 1
                ch = 4 * g + cc
                P.op("act", "activation", out=epi.st16[i][:], in_=pap, func=AF.Identity, scale=psc[:, ch:ch + 1],
                     reads=[pbuf, b_sc], writes=[epi.b16[i]])
                P.dma("sp", C.S_mix[16 + ch, :, t0:t0 + 512], epi.st16[i][:], epi.b16[i], C.B_mix, False)

            ep.setup = epi.setup
            ep.act_buf = C.B_pl
            allc = set(range(4))
            gemm(C, "plg%d" % g, C.pool_w[g], 512, 512, C.S_pl[4 * g:4 * g + 4], [0, 1024], 2, lambda t0: allc, ep)


def pf(v, n):
    return np.ascontiguousarray(np.asarray(v, np.float32).reshape(n, 128).T)


def t5_bucket_np(dist):
    max_exact = 16
    nf = np.maximum(dist, 1).astype(np.float32)
    large = max_exact + (np.log(nf / np.float32(max_exact)) / np.float32(np.log(2048 / max_exact))
                         * np.float32(32 - max_exact)).astype(np.int32)
    large = np.minimum(large, 31)
    return np.where(dist < max_exact, dist, large)


def make_biasmat(rel_bias):
    k = np.arange(128)[:, None, None]
    kb = np.arange(2)[None, :, None]
    q = np.arange(128)[None, None, :]
    step = q + 128 - (kb * 128 + k)
    valid = (step >= 0) & (step <= 128)
    out = np.full((16, 128, 3, 2, 128), -30000.0, np.float32)
    for di, dl in enumerate((1, 4, 16)):
        bucket = t5_bucket_np(np.clip(step, 0, 128) * dl)
        for h in range(16):
            vals = rel_bias[bucket, h]
            out[h, :, di] = np.where(valid, vals, np.float32(-30000.0))
    return out.reshape(16, 128, 3 * 256)


_NC_CACHE = {}
_DECL = {}


def nc_declared(nc):
    return _DECLS.get(id(nc), set())


_DECLS = {}


def layer_inputs(layer, inp, sfx):
    l = layer
    sh = {
        "w_in" + sfx: np.ascontiguousarray(inp["w_in"][l]),
        "w_out" + sfx: np.ascontiguousarray(inp["w_out"][l]),
        "w1" + sfx: np.ascontiguousarray(inp["mlp_w1"][l]),
        "w2" + sfx: np.ascontiguousarray(inp["mlp_w2"][l]),
        "ln_g" + sfx: pf(inp["ln_g"][l].reshape(-1), 2 * NCH),
        "ln_b" + sfx: pf(inp["ln_b"][l].reshape(-1), 2 * NCH),
    }
    if layer == 0:
        cw = np.asarray(inp["conv_w"][0], np.float32)
        sh["conv_w"] = np.ascontiguousarray(cw.T.reshape(16, 128, CONV_K).transpose(1, 0, 2).reshape(128, 16 * CONV_K))
        sh["conv_b"] = pf(inp["conv_b"][0], 16)
        sh["conv_g"] = pf(inp["conv_ln_g"][0], 16)
        sh["conv_bb"] = pf(inp["conv_ln_b"][0], 16)
        sh["biasmat"] = make_biasmat(np.asarray(inp["rel_bias"], np.float32))
    else:
        sh["lbl"] = pf(np.asarray(inp["hgrn_lb_logits"], np.float32).reshape(-1), 32)
        sh["hng"] = pf(inp["hgrn_norm_g"][0], 16)
        sh["pool_w"] = np.ascontiguousarray(inp["pool_w"][0])
        sh["pool_s"] = pf(inp["pool_scale"][0], 16)
        sh["cmask"] = np.triu(np.ones((64, 64), np.float32))
    return sh


def run_prog(layers, xfull, inp, phases=None, ncores=8):
    key = (tuple(layers), tuple(phases) if phases else None)
    if key not in _NC_CACHE:
        _NC_CACHE[key] = build_prog(tuple(layers), phases)
    nc = _NC_CACHE[key]
    fused = len(layers) > 1
    shared = {"ident": np.eye(128, dtype=np.float32)}
    for l in layers:
        shared.update(layer_inputs(l, inp, str(l) if fused else ""))
    in_maps = []
    for core in range(8):
        b, half = core // 2, core % 2
        xe = np.zeros((TE, D), np.float32)
        if half == 1:
            xe[:] = xfull[b]
        else:
            xe[T:] = xfull[b, :T]
        m = dict(shared)
        m["xe"] = xe
        m["flag"] = np.full((128, 1), float(half), np.float32)
        m["c_pf"] = pf(inp["c"][b], NCH)
        for l in layers:
            sfx = str(l) if fused else ""
            hw = 3 * D
            m["ada_w" + sfx] = np.ascontiguousarray(inp["ada_w"][l][:, half * hw:(half + 1) * hw])
            m["ada_b" + sfx] = np.ascontiguousarray(pf(inp["ada_b"][l], 6 * NCH)[:, half * 3 * NCH:(half + 1) * 3 * NCH])
        if 1 in layers:
            tabs = np.arange(T) + half * T
            ic = np.stack([1.0 / np.minimum(tabs + 1, w).astype(np.float32) for w in (2, 4, 8, 16)]).astype(np.float32)
            m["invcnt"] = np.ascontiguousarray(np.broadcast_to(ic.reshape(1, 4 * T), (128, 4 * T)))
        in_maps.append(m)
    in_maps = [{k: v for k, v in m.items() if k in nc_declared(nc)} for m in in_maps[:ncores]]
    res = run_bass_kernel_spmd(nc, in_maps, core_ids=list(range(ncores)))
    LAST_RES[0] = res
    if ncores < 8 or (phases is not None and "ln1" not in phases):
        return None
    out = np.empty((NB, SEQ, D), np.float32)
    for core in range(8):
        b, half = core // 2, core % 2
        out[b, half * T:(half + 1) * T] = res.results[core]["y"]
    return out


def run_layer(layer, xfull, inp, phases=None, ncores=8):
    return run_prog((layer,), xfull, inp, phases, ncores)


def kernel(**inputs):
    inp = {k: np.asarray(v) for k, v in inputs.items()}
    x = np.asarray(inp["x"], np.float32)
    return run_prog((0, 1), x, inp)
```

```python
import numpy as np
from contextlib import ExitStack
import concourse.bass as bass
import concourse.mybir as mybir
from concourse.bass_utils import run_bass_kernel_spmd

F32 = mybir.dt.float32
BF16 = mybir.dt.bfloat16
AF = mybir.ActivationFunctionType
ALU = mybir.AluOpType

D = 4096
SEQ = 4096
NB = 4
T = 2048
TE = 4096
MIX = 2048
NCH = D // 128
ALPHA = 4.0 ** 0.25
EPS = 1e-5
CONV_K = 31
HALO = 128
SAME_ENG_SYNC = True
DEBUG_OUT = set()
LAST_RES = [None]


class Buf:
    __slots__ = ("name", "w", "r", "dsem", "excl")

    def __init__(self, name, excl=False):
        self.name = name
        self.excl = excl
        self.w = {}
        self.r = {}
        self.dsem = None


class Prog:
    ENGS = ("pe", "act", "dve", "pool", "sp")

    def __init__(self, nc, es, n_dma_sems=90):
        self.nc = nc
        self.ops = {e: [] for e in self.ENGS}
        self.sems = {}
        self.cnt = {}
        for e in self.ENGS:
            self.sems["E" + e] = es.enter_context(nc.semaphore("prog_" + e))
            self.cnt["E" + e] = 0
        self.free_dsems = []
        for i in range(n_dma_sems):
            k = "D%d" % i
            self.sems[k] = es.enter_context(nc.semaphore("dma_%d" % i))
            self.cnt[k] = 0
            self.free_dsems.append(k)
        self.sems["CC"] = es.enter_context(nc.semaphore("coll_cc"))
        self.cnt["CC"] = 0
        self.waited = {e: {} for e in self.ENGS}
        self.bufs = []

    def buf(self, name, excl=False):
        b = Buf(name, excl)
        self.bufs.append(b)
        return b

    def bufs_n(self, name, n, excl=False):
        return [self.buf("%s%d" % (name, i), excl) for i in range(n)]

    def pbuf(self, name):
        return self.buf(name, True)

    def pbufs_n(self, name, n):
        return self.bufs_n(name, n, True)

    def _waits(self, eng, reads, writes):
        need = {}
        for b in reads:
            for k, v in b.w.items():
                if need.get(k, 0) < v:
                    need[k] = v
            if b.excl:
                for k, v in b.r.items():
                    if need.get(k, 0) < v:
                        need[k] = v
        for b in writes:
            for k, v in b.w.items():
                if need.get(k, 0) < v:
                    need[k] = v
            for k, v in b.r.items():
                if need.get(k, 0) < v:
                    need[k] = v
        out = []
        wd = self.waited[eng]
        for k, v in need.items():
            if k == "E" + eng and (eng == "pe" or eng == "sp" or not SAME_ENG_SYNC):
                continue
            if wd.get(k, 0) >= v:
                continue
            wd[k] = v
            out.append((k, v))
        return out

    def op(self, eng, meth, *args, reads=(), writes=(), sig=True, **kw):
        waits = self._waits(eng, reads, writes)
        k = "E" + eng
        if not sig:
            self.ops[eng].append((waits, meth, args, kw, None))
            return
        self.cnt[k] += 1
        v = self.cnt[k]
        self.ops[eng].append((waits, meth, args, kw, (k, 1)))
        for b in writes:
            b.w = {k: v}
            b.r = {}
        for b in reads:
            if b.excl:
                b.w = {k: v}
                b.r = {}
            elif b.r.get(k, 0) < v:
                b.r[k] = v

    def dma(self, q, out, in_, sb, dram, load, extra=(), **kw):
        if sb.dsem is None:
            sb.dsem = self.free_dsems.pop()
        k = sb.dsem
        if load:
            reads, writes = ([dram] if dram is not None else []), [sb]
        else:
            reads, writes = [sb] + list(extra), ([dram] if dram is not None else [])
        waits = self._waits(q, reads, writes)
        self.cnt[k] += 16
        v = self.cnt[k]
        self.ops[q].append((waits, "dma_start", (), dict(out=out, in_=in_, **kw), (k, 16)))
        for b in writes:
            if b is sb:
                b.w = {k: v}
                b.r = {}
            else:
                b.w[k] = v
        for b in reads:
            if b.r.get(k, 0) < v:
                b.r[k] = v

    def coll(self, kind, alu, groups, in_ap, out_ap, b_in, b_out):
        k = "CC"
        if k not in self.sems:
            raise RuntimeError("no CC sem")
        waits = self._waits("pool", [b_in], [b_out])
        self.cnt[k] += 1
        v = self.cnt[k]
        self.ops["pool"].append((waits, "collective_compute", (kind, alu),
                                 dict(replica_groups=groups, ins=[in_ap], outs=[out_ap]), (k, 1)))
        b_out.w[k] = v
        if b_in.r.get(k, 0) < v:
            b_in.r[k] = v

    def barrier(self):
        for e in self.ENGS:
            waits = []
            wd = self.waited[e]
            for k, v in self.cnt.items():
                if v == 0 or k == "E" + e:
                    continue
                if wd.get(k, 0) >= v:
                    continue
                wd[k] = v
                waits.append((k, v))
            if waits:
                self.ops[e].append((waits, None, (), {}, None))
        for b in self.bufs:
            b.w = {}
            b.r = {}
            if b.dsem is not None:
                self.free_dsems.append(b.dsem)
                b.dsem = None
        self.bufs = [b for b in self.bufs if getattr(b, "keep", False) or True]

    def emit(self):
        nc = self.nc
        engmap = {"pe": "tensor", "act": "scalar", "dve": "vector", "pool": "gpsimd", "sp": "sync"}
        with nc.Block() as block:
            for e in self.ENGS:
                ops = self.ops[e]
                sems = self.sems

                def body(eng, ops=ops, sems=sems):
                    for waits, meth, args, kw, inc in ops:
                        for k, v in waits:
                            eng.wait_ge(sems[k], v)
                        if meth is None:
                            continue
                        ins = getattr(eng, meth)(*args, **kw)
                        if inc is not None:
                            ins.then_inc(sems[inc[0]], inc[1])

                getattr(block, engmap[e])(body)


class Ctx:
    pass


_UID = [0]


def sb(nc, es, name, shape, dt):
    _UID[0] += 1
    return es.enter_context(nc.sbuf_tensor("%s_u%d" % (name, _UID[0]), list(shape), dt))


def ps(nc, es, name, shape, dt=F32):
    _UID[0] += 1
    return es.enter_context(nc.psum_tensor("%s_u%d" % (name, _UID[0]), list(shape), dt))


PAIR_GROUPS = [[0, 1], [2, 3], [4, 5], [6, 7]]
XCH = 128


def build_prog(layers, phases=None):
    nc = bass.Bass("TRN2", target_bir_lowering=False)
    es = ExitStack()
    C = Ctx()
    C.nc = nc
    fused = len(layers) > 1
    dt_sc = lambda name, shape, dt=F32: nc.dram_tensor(
        name, list(shape), dt, kind=("ExternalOutput" if name in DEBUG_OUT else "Internal")).ap()
    if phases is None:
        phases = ["mod", "xT", "win", "mix", "wout", "ln0", "mlp1", "mlp2", "ln1"]
    C.declared = set()

    def dt_in(name, shape, dt=F32, ph=None):
        if ph is not None and not (set(ph) & set(phases)):
            return None
        C.declared.add(name)
        return nc.dram_tensor(name, list(shape), dt, kind="ExternalInput").ap()

    C.xe = dt_in("xe", [TE, D], ph=["xT"])
    C.flag = dt_in("flag", [128, 1])
    C.c_pf = dt_in("c_pf", [128, NCH])
    C.ident = dt_in("ident", [128, 128])
    C.y = nc.dram_tensor("y", [T, D], F32, kind="ExternalOutput").ap()
    C.S_xT = dt_sc("S_xT", [NCH, 128, T])
    C.S_hT = dt_sc("S_hT", [NCH, 128, TE], BF16)
    C.S_u32 = dt_sc("S_u32", [80, 128, TE])
    C.S_u16 = dt_sc("S_u16", [80, 128, TE], BF16)
    C.S_mix = dt_sc("S_mix", [NCH, 128, T], BF16)
    C.S_z = dt_sc("S_z", [NCH, 128, T])
    C.S_z2 = dt_sc("S_z2", [NCH, 128, T])
    C.S_h2 = dt_sc("S_h2", [NCH, 128, T], BF16)
    C.S_hid = dt_sc("S_hid", [128, 128, T], BF16)
    C.S_cv = dt_sc("S_cv", [16, 128, T])
    C.S_pl = dt_sc("S_pl", [16, 128, T], BF16)
    C.Mx = dt_sc("Mx", [128, 3 * NCH])
    C.Mg = dt_sc("Mg", [256, 3 * NCH])
    if fused:
        C.X1 = dt_sc("X1", [T, D])
        C.G = dt_sc("G", [T // XCH, 2 * XCH, D])

    P = Prog(nc, es)
    C.P = P
    for n in ("xT", "hT", "u", "mix", "z", "z2", "h2", "hid", "cv", "pl", "y", "x1", "G", "Mx", "Mg"):
        setattr(C, "B_" + n, P.buf("dram_" + n))

    C.ident_t = sb(nc, es, "ident_t", [128, 128], F32)
    C.identb_t = sb(nc, es, "identb_t", [128, 128], BF16)
    C.ones_t = sb(nc, es, "ones_t", [128, 128], F32)
    C.onesb_t = sb(nc, es, "onesb_t", [128, 128], BF16)
    C.flag_t = sb(nc, es, "flag_t", [128, 1], F32)
    C.mod_t = sb(nc, es, "mod_t", [128, 6 * NCH], F32)
    C.vec_t = sb(nc, es, "vec_t", [128, 10 * NCH], F32)
    C.lng_t = sb(nc, es, "lng_t", [128, 2 * NCH], F32)
    C.lnb_t = sb(nc, es, "lnb_t", [128, 2 * NCH], F32)
    C.B_const = P.buf("const")
    phase_setup(C)

    for li, layer in enumerate(layers):
        C.layer = layer
        sfx = str(layer) if fused else ""
        C.ada_w = dt_in("ada_w" + sfx, [D, 3 * D], ph=["mod"])
        C.ada_b = dt_in("ada_b" + sfx, [128, 3 * NCH])
        C.w_in = dt_in("w_in" + sfx, [D, 5 * MIX], ph=["win"])
        C.w_out = dt_in("w_out" + sfx, [D, D], ph=["wout"])
        C.w1 = dt_in("w1" + sfx, [D, 4 * D], ph=["mlp1"])
        C.w2 = dt_in("w2" + sfx, [4 * D, D], ph=["mlp2"])
        C.ln_g = dt_in("ln_g" + sfx, [128, 2 * NCH])
        C.ln_b = dt_in("ln_b" + sfx, [128, 2 * NCH])
        if layer == 0:
            C.conv_w = dt_in("conv_w", [128, 16 * CONV_K])
            C.conv_b = dt_in("conv_b", [128, 16])
            C.conv_g = dt_in("conv_g", [128, 16])
            C.conv_bb = dt_in("conv_bb", [128, 16])
            C.biasmat = dt_in("biasmat", [16, 128, 3 * 256], ph=["mix"])
        else:
            C.lbl = dt_in("lbl", [128, 2 * 16])
            C.hng = dt_in("hng", [128, 16])
            C.pool_w = dt_in("pool_w", [4, 512, 512], ph=["mix"])
            C.pool_s = dt_in("pool_s", [128, 16])
            C.invcnt = dt_in("invcnt", [128, 4 * T], ph=["mix"])
            C.cmask = dt_in("cmask", [64, 64])
        first, last = (li == 0), (li == len(layers) - 1)
        if first:
            C.xe_rows = lambda r0: (C.xe[r0:r0 + 128, :], None)
        else:
            C.xe_rows = lambda r0: ((C.G[r0 // XCH][0:128, :], C.B_G) if r0 < T
                                    else (C.X1[r0 - T:r0 - T + 128, :], C.B_x1))
        if last:
            C.out_rows = lambda r0: (C.y[r0:r0 + 128, :], C.B_y)
        else:
            C.out_rows = lambda r0: (C.X1[r0:r0 + 128, :], C.B_x1)
        bc = C.B_const
        P.dma("sp", C.lng_t[:], C.ln_g[:, :], bc, None, True)
        P.dma("sp", C.lnb_t[:], C.ln_b[:, :], bc, None, True)
        P.barrier()
        if "mod" in phases:
            phase_mod(C)
        if "xT" in phases:
            phase_xT(C)
        if "win" in phases:
            phase_win(C)
        if "mix" in phases:
            if layer == 0:
                phase_conv(C)
                phase_attn(C)
            else:
                phase_hgrn(C)
                phase_pool(C)
        if "wout" in phases:
            phase_wout(C)
        if "ln0" in phases:
            phase_ln(C, which=0)
        if "mlp1" in phases:
            phase_mlp1(C)
        if "mlp2" in phases:
            phase_mlp2(C)
        if "ln1" in phases:
            phase_ln(C, which=1)
        if not last:
            phase_exchange(C)
    P.barrier()
    P.emit()
    es.close()
    _DECLS[id(nc)] = C.declared
    return nc


def build_layer(layer, phases=None):
    return build_prog((layer,), phases)


def phase_exchange(C):
    P = C.P
    P.barrier()
    for i in range(T // XCH):
        P.coll("AllGather", ALU.bypass, PAIR_GROUPS, C.X1[i * XCH:(i + 1) * XCH, :], C.G[i], C.B_x1, C.B_G)
    P.barrier()


V_A1, V_AG0, V_AB0, V_A2, V_B2, V_TMP = 0, 1, 2, 3, 4, 5
M_SH1, M_SC1, M_G1, M_SH2, M_SC2, M_G2 = 0, 1, 2, 3, 4, 5


def vcol(t, j, fc):
    return t[:, j * NCH + fc: j * NCH + fc + 1]


def phase_setup(C):
    P, nc = C.P, C.nc
    bc = C.B_const
    P.dma("sp", C.ident_t[:], C.ident[:, :], bc, None, True)
    P.dma("sp", C.flag_t[:], C.flag[:, :], bc, None, True)
    P.op("dve", "memset", C.ones_t[:], 1.0, writes=[bc])
    P.op("dve", "memset", C.onesb_t[:], 1.0, writes=[bc])
    P.op("dve", "tensor_copy", C.identb_t[:], C.ident_t[:], reads=[bc], writes=[bc])
    P.barrier()


def phase_mod(C):
    P, nc, l = C.P, C.nc, C.layer
    HC = 3 * NCH
    with ExitStack() as es:
        c_t = sb(nc, es, "c_t", [128, NCH], F32)
        cs_t = sb(nc, es, "cs_t", [128, NCH], F32)
        ab_t = sb(nc, es, "ab_t", [128, HC], F32)
        mh_t = sb(nc, es, "mh_t", [128, HC], F32)
        W = [sb(nc, es, "adaW%d" % i, [128, NCH, 512], F32) for i in range(2)]
        mps = ps(nc, es, "modps", [128, 512])
        b_c, b_ab, b_ps, b_mh = P.buf("c"), P.buf("ab"), P.pbuf("modps"), P.buf("mh")
        b_W = P.bufs_n("adaW", 2)
        P.dma("sp", c_t[:], C.c_pf[:, :], b_c, None, True)
        P.dma("sp", ab_t[:], C.ada_b[:, :], b_ab, None, True)
        P.op("act", "activation", out=cs_t[:], in_=c_t[:], func=AF.Silu, reads=[b_c], writes=[b_c])
        wv = C.ada_w.rearrange("(kc p) n -> p kc n", p=128)
        for cg in range(HC // 4):
            s = cg % 2
            for part in range(4):
                q = "sp" if part % 2 == 0 else "pool"
                P.dma(q, W[s][:, part * 8:(part + 1) * 8, :], wv[:, part * 8:(part + 1) * 8, cg * 512:(cg + 1) * 512],
                      b_W[s], None, True)
            for cc in range(4):
                col = cg * 4 + cc
                for kc in range(NCH):
                    P.op("pe", "matmul", mps[:, col:col + 1], W[s][:, kc, cc * 128:(cc + 1) * 128], cs_t[:, kc:kc + 1],
                         start=(kc == 0), stop=(kc == NCH - 1), reads=[b_W[s], b_c], writes=[b_ps],
                         sig=(kc == NCH - 1))
        bc = C.B_const
        P.op("dve", "tensor_tensor", mh_t[:], mps[:, 0:HC], ab_t[:], ALU.add, reads=[b_ps, b_ab], writes=[b_mh])
        P.dma("sp", C.Mx[:, :], mh_t[:], b_mh, C.B_Mx, False)
        P.coll("AllGather", ALU.bypass, PAIR_GROUPS, C.Mx[:, :], C.Mg[:, :], C.B_Mx, C.B_Mg)
        P.dma("sp", C.mod_t[:, 0:HC], C.Mg[0:128, :], bc, C.B_Mg, True)
        P.dma("sp", C.mod_t[:, HC:2 * HC], C.Mg[128:256, :], bc, C.B_Mg, True)
        m, v = C.mod_t, C.vec_t
        sl = lambda t, j: t[:, j * NCH:(j + 1) * NCH]
        P.op("dve", "tensor_scalar", sl(v, V_A1), sl(m, M_SC1), 1.0, None, ALU.add, reads=[bc], writes=[bc])
        P.op("dve", "tensor_scalar", sl(v, V_AG0), C.lng_t[:, 0:NCH], ALPHA, None, ALU.mult, reads=[bc], writes=[bc])
        P.op("dve", "tensor_scalar", sl(v, V_AB0), C.lnb_t[:, 0:NCH], ALPHA, None, ALU.mult, reads=[bc], writes=[bc])
        P.op("dve", "tensor_scalar", sl(v, V_TMP), sl(m, M_SC2), 1.0, None, ALU.add, reads=[bc], writes=[bc])
        P.op("dve", "tensor_tensor", sl(v, V_A2), sl(v, V_TMP), C.lng_t[:, 0:NCH], ALU.mult, reads=[bc], writes=[bc])
        P.op("dve", "tensor_tensor", sl(v, V_B2), sl(v, V_TMP), C.lnb_t[:, 0:NCH], ALU.mult, reads=[bc], writes=[bc])
        P.op("dve", "tensor_tensor", sl(v, V_B2), sl(v, V_B2), sl(m, M_SH2), ALU.add, reads=[bc], writes=[bc])
        P.barrier()


def phase_xT(C):
    import os
    XM = int(os.environ.get("XT_MODE", "9"))
    P, nc = C.P, C.nc
    TB = 256
    with ExitStack() as es:
        xin = [sb(nc, es, "xin%d" % i, [128, 2, D], F32) for i in range(2)]
        hT = [sb(nc, es, "hTo%d" % i, [128, NCH, TB], BF16) for i in range(2)]
        xT = [sb(nc, es, "xTo%d" % i, [128, NCH, TB], F32) for i in range(2)]
        pst_ = [ps(nc, es, "xTps%d" % i, [128, 512]) for i in range(4)]
        pst = [p[:, :].rearrange("p (a b) -> p a b", a=4) for p in pst_]
        b_xin, b_hT, b_xT, b_ps = P.bufs_n("xin", 2), P.bufs_n("hTo", 2), P.bufs_n("xTo", 2), P.pbufs_n("xTps", 4)
        b_hTd, b_xTd = P.bufs_n("hTod", 2), P.bufs_n("xTod", 2)
        hview = C.S_hT.rearrange("c p t -> p c t")
        xview = C.S_xT.rearrange("c p t -> p c t")
        pc = 0
        for tb in range(TE // TB):
            s = tb % 2
            own = tb * TB >= T
            for j in range(2):
                src_ap, src_buf = C.xe_rows(tb * TB + j * 128)
                P.dma("sp", xin[s][:, j, :], src_ap, b_xin[s], src_buf, True)
            for j in range(2):
                for g in range(NCH // 4):
                    pb = pc % 4
                    pc += 1
                    for i in range(4):
                        fc = g * 4 + i
                        if XM < 1:
                            continue
                        P.op("pe", "transpose", pst[pb][:, i, :], xin[s][:, j, fc * 128:(fc + 1) * 128], C.ident_t[:],
                             reads=[b_xin[s], C.B_const], writes=[b_ps[pb]])
                    on_act = (g % 2 == 0)
                    for i in range(4):
                        fc = g * 4 + i
                        if XM < 2:
                            continue
                        if on_act:
                            P.op("act", "activation", out=hT[s][:, fc, j * 128:(j + 1) * 128], in_=pst[pb][:, i, :],
                                 func=AF.Identity, scale=vcol(C.vec_t, V_A1, fc), bias=vcol(C.mod_t, M_SH1, fc),
                                 reads=[b_ps[pb], C.B_const], writes=[b_hT[s]])
                        else:
                            P.op("dve", "tensor_scalar", hT[s][:, fc, j * 128:(j + 1) * 128], pst[pb][:, i, :],
                                 vcol(C.vec_t, V_A1, fc), vcol(C.mod_t, M_SH1, fc), ALU.mult, ALU.add,
                                 reads=[b_ps[pb], C.B_const], writes=[b_hTd[s]])
                    if own and XM >= 3:
                        if on_act:
                            P.op("act", "activation", out=xT[s][:, g * 4:(g + 1) * 4, j * 128:(j + 1) * 128],
                                 in_=pst[pb][:, :, :], func=AF.Identity, scale=ALPHA,
                                 reads=[b_ps[pb]], writes=[b_xT[s]])
                        else:
                            P.op("dve", "tensor_scalar", xT[s][:, g * 4:(g + 1) * 4, j * 128:(j + 1) * 128],
                                 pst[pb][:, :, :], ALPHA, None, ALU.mult, reads=[b_ps[pb]], writes=[b_xTd[s]])
            if XM >= 4:
                P.dma("sp", hview[:, :, tb * TB:(tb + 1) * TB], hT[s][:], b_hT[s], C.B_hT, False, extra=[b_hTd[s]])
            if own and XM >= 5:
                t0 = tb * TB - T
                P.dma("sp", xview[:, :, t0:t0 + TB], xT[s][:], b_xT[s], C.B_xT, False, extra=[b_xTd[s]])
        P.barrier()


def gemm(C, name, W, K, N, actT, tiles, ntt, colsel, epilogue, wcols=512):
    P, nc = C.P, C.nc
    KC = K // 128
    KG = max(1, KC // 32)
    kcg = min(KC, 32)
    with ExitStack() as es:
        NCG = wcols // 128
        nact = 2 if (2 * KC * 512 * ntt * 2 + 2 * kcg * wcols * 2) <= 150 * 1024 else 1
        act = [sb(nc, es, "%s_act%d" % (name, i), [128, KC, 512 * ntt], BF16) for i in range(nact)]
        Wt = [sb(nc, es, "%s_W%d" % (name, i), [128, kcg, wcols], BF16) for i in range(2)]
        pbank = [ps(nc, es, "%s_ps%d" % (name, i), [128, 512]) for i in range(8)]
        b_act, b_W, b_ps = P.bufs_n(name + "act", nact), P.bufs_n(name + "W", 2), P.pbufs_n(name + "ps", 8)
        st = epilogue.setup(es) if hasattr(epilogue, "setup") else None
        wv = W.rearrange("(kc p) n -> p kc n", p=128)
        av = actT.rearrange("c p t -> p c t")
        wcount = 0
        bcount = 0
        for ti, t0 in enumerate(tiles):
            a = ti % nact
            for part in range(max(1, KC // 16)):
                k0, k1 = part * 16, min(KC, (part + 1) * 16)
                P.dma("sp", act[a][:, k0:k1, :], av[:, k0:k1, t0:t0 + 512 * ntt], b_act[a], epilogue.act_buf, True)
            sel = colsel(t0)
            for cg in range(N // wcols):
                chunks = [cc for cc in range(cg * NCG, cg * NCG + NCG) if cc in sel]
                if not chunks:
                    continue
                if KG > 1:
                    banks = {cc: (bcount % (8 // NCG)) * NCG + (cc - cg * NCG) for cc in chunks}
                    bcount += 1
                for kg in range(KG):
                    ws = wcount % 2
                    wcount += 1
                    nparts = 4 if kcg >= 32 else 1
                    step = kcg // nparts
                    for part in range(nparts):
                        P.dma("pool", Wt[ws][:, part * step:(part + 1) * step, :],
                              wv[:, kg * 32 + part * step: kg * 32 + (part + 1) * step, cg * wcols:(cg + 1) * wcols],
                              b_W[ws], None, True)
                    for cc in chunks:
                        cl = cc - cg * NCG
                        for s in range(ntt):
                            if KG > 1:
                                bk = banks[cc]
                            else:
                                bk = bcount % 8
                                bcount += 1
                            for kc in range(kcg):
                                P.op("pe", "matmul", pbank[bk][:, :], Wt[ws][:, kc, cl * 128:(cl + 1) * 128],
                                     act[a][:, kg * 32 + kc, s * 512:(s + 1) * 512],
                                     start=(kg == 0 and kc == 0), stop=(kg == KG - 1 and kc == kcg - 1),
                                     reads=[b_W[ws], b_act[a]], writes=[b_ps[bk]], sig=(kc == kcg - 1))
                            if kg == KG - 1:
                                epilogue(cc, t0 + s * 512, pbank[bk][:, :], b_ps[bk])
        P.barrier()


class EpiStore:
    def __init__(self, C, act_buf, nslot=4):
        self.C = C
        self.act_buf = act_buf
        self.n = 0
        self.ns = nslot

    def setup(self, es):
        C = self.C
        n = self.ns
        self.st32 = [sb(C.nc, es, "epi32_%d" % i, [128, 512], F32) for i in range(n)]
        self.st16 = [sb(C.nc, es, "epi16_%d" % i, [128, 512], BF16) for i in range(n)]
        self.b32 = C.P.bufs_n("epi32_", n)
        self.b16 = C.P.bufs_n("epi16_", n)
        self.aux = [sb(C.nc, es, "epiaux_%d" % i, [128, 512], F32) for i in range(n)]
        self.baux = C.P.bufs_n("epiaux_", n)


def phase_win(C):
    P, layer = C.P, C.layer
    if layer == 0:
        kv = set(range(48, 80))
        halo = set(range(0, 32))
        f32c = set(range(0, 32))
    else:
        kv = set(range(16, 48))
        halo = set(range(64, 80))
        f32c = set(range(0, 32)) | set(range(48, 80))
    allc = set(range(80))

    def colsel(t0):
        if t0 >= T:
            return allc
        if t0 + 1024 >= T:
            return kv | halo
        return kv

    epi = EpiStore(C, C.B_hT)

    def ep(cc, t0, pap, pbuf):
        i = epi.n % epi.ns
        epi.n += 1
        if cc in f32c:
            P.op("act", "activation", out=epi.st32[i][:], in_=pap, func=AF.Copy, reads=[pbuf], writes=[epi.b32[i]])
            P.dma("sp", C.S_u32[cc, :, t0:t0 + 512], epi.st32[i][:], epi.b32[i], C.B_u, False)
        else:
            P.op("act", "activation", out=epi.st16[i][:], in_=pap, func=AF.Copy, reads=[pbuf], writes=[epi.b16[i]])
            P.dma("sp", C.S_u16[cc, :, t0:t0 + 512], epi.st16[i][:], epi.b16[i], C.B_u, False)

    ep.setup = epi.setup
    ep.act_buf = C.B_hT
    gemm(C, "win", C.w_in, D, 5 * MIX, C.S_hT, [0, 1024, 2048, 3072], 2, colsel, ep)


def phase_wout(C, ):
    res_gemm(C, "wout", C.w_out, D, C.S_mix, C.B_mix, M_G1, ntt=2)


def res_gemm(C, name, W, K, actT, act_buf, gidx, ntt, wcols=512, nslot=4, src=None, dst=None):
    P = C.P
    epi = EpiStore(C, act_buf, nslot)
    src_ap, src_buf = src if src is not None else (C.S_xT, C.B_xT)
    dst_ap, dst_buf = dst if dst is not None else (C.S_z, C.B_z)

    def ep(cc, t0, pap, pbuf):
        i = epi.n % epi.ns
        epi.n += 1
        P.dma("sp", epi.aux[i][:], src_ap[cc, :, t0:t0 + 512], epi.baux[i], src_buf, True)
        P.op("dve", "scalar_tensor_tensor", epi.st32[i][:], pap, vcol(C.mod_t, gidx, cc), epi.aux[i][:],
             ALU.mult, ALU.add, reads=[pbuf, epi.baux[i], C.B_const], writes=[epi.b32[i]])
        P.dma("sp", dst_ap[cc, :, t0:t0 + 512], epi.st32[i][:], epi.b32[i], dst_buf, False)

    ep.setup = epi.setup
    ep.act_buf = act_buf
    allc = set(range(NCH))
    tiles = [i * 512 * ntt for i in range(T // (512 * ntt))]
    gemm(C, name, W, K, D, actT, tiles, ntt, lambda t0: allc, ep, wcols=wcols)


def phase_mlp1(C):
    P = C.P
    epi = EpiStore(C, C.B_h2)

    def ep(cc, t0, pap, pbuf):
        i = epi.n % epi.ns
        epi.n += 1
        P.op("act", "activation", out=epi.st32[i][:], in_=pap, func=AF.Relu, reads=[pbuf], writes=[epi.b32[i]])
        P.op("dve", "tensor_tensor", epi.st16[i][:], epi.st32[i][:], epi.st32[i][:], ALU.mult,
             reads=[epi.b32[i]], writes=[epi.b16[i]])
        P.dma("sp", C.S_hid[cc, :, t0:t0 + 512], epi.st16[i][:], epi.b16[i], C.B_hid, False)

    ep.setup = epi.setup
    ep.act_buf = C.B_h2
    allc = set(range(128))
    gemm(C, "mlp1", C.w1, D, 4 * D, C.S_h2, [0, 1024], 2, lambda t0: allc, ep)


def phase_mlp2(C):
    zs = [(C.S_xT, C.B_xT), (C.S_z, C.B_z), (C.S_z2, C.B_z2), (C.S_z, C.B_z), (C.S_z2, C.B_z2)]
    for kg in range(4):
        res_gemm(C, "mlp2_%d" % kg, C.w2[kg * D:(kg + 1) * D, :], D, C.S_hid[kg * NCH:(kg + 1) * NCH], C.B_hid,
                 M_G2, ntt=2, src=zs[kg], dst=zs[kg + 1])


def ln_stats(C, es, name, zt, nch, nfeat, b_z, width=512):
    P, nc = C.P, C.nc
    sq = [sb(nc, es, name + "_sq%d" % i, [128, width], F32) for i in range(2)]
    b_sq = P.bufs_n(name + "sq", 2)
    p_sum = ps(nc, es, name + "_psum", [128, width])
    p_sq = ps(nc, es, name + "_psq", [128, width])
    b_ps = P.pbuf(name + "pss")
    mean = sb(nc, es, name + "_mean", [128, width], F32)
    msq = sb(nc, es, name + "_msq", [128, width], F32)
    rstd = sb(nc, es, name + "_rstd", [128, width], F32)
    b_st = P.buf(name + "stat")
    return dict(sq=sq, b_sq=b_sq, p_sum=p_sum, p_sq=p_sq, b_ps=b_ps, mean=mean, msq=msq, rstd=rstd, b_st=b_st,
                nch=nch, nfeat=nfeat)


def ln_stats_run(C, S, zt, b_z):
    P = C.P
    nch = S["nch"]
    for fc in range(nch):
        i = fc % 2
        P.op("act", "activation", out=S["sq"][i][:], in_=zt[:, fc, :], func=AF.Square, reads=[b_z], writes=[S["b_sq"][i]])
        P.op("pe", "matmul", S["p_sum"][:], C.ones_t[:], zt[:, fc, :], start=(fc == 0), stop=(fc == nch - 1),
             reads=[b_z, C.B_const], writes=[S["b_ps"]])
        P.op("pe", "matmul", S["p_sq"][:], C.ones_t[:], S["sq"][i][:], start=(fc == 0), stop=(fc == nch - 1),
             reads=[S["b_sq"][i], C.B_const], writes=[S["b_ps"]])
    inv = 1.0 / S["nfeat"]
    b_st = S["b_st"]
    P.op("dve", "tensor_scalar", S["mean"][:], S["p_sum"][:], inv, None, ALU.mult, reads=[S["b_ps"]], writes=[b_st])
    P.op("dve", "tensor_scalar", S["msq"][:], S["p_sq"][:], inv, None, ALU.mult, reads=[S["b_ps"]], writes=[b_st])
    P.op("dve", "tensor_tensor", S["rstd"][:], S["mean"][:], S["mean"][:], ALU.mult, reads=[b_st], writes=[b_st])
    P.op("dve", "tensor_tensor", S["msq"][:], S["msq"][:], S["rstd"][:], ALU.subtract, reads=[b_st], writes=[b_st])
    P.op("dve", "tensor_scalar", S["msq"][:], S["msq"][:], EPS, None, ALU.add, reads=[b_st], writes=[b_st])
    P.op("act", "activation", out=S["msq"][:], in_=S["msq"][:], func=AF.Sqrt, reads=[b_st], writes=[b_st])
    P.op("dve", "reciprocal", S["rstd"][:], S["msq"][:], reads=[b_st], writes=[b_st])


def phase_ln(C, which):
    P, nc = C.P, C.nc
    with ExitStack() as es:
        zt = sb(nc, es, "ln_z", [128, NCH, 512], F32)
        b_z = P.buf("ln_z")
        S = ln_stats(C, es, "ln", zt, NCH, D, b_z)
        if which == 0:
            o32 = [sb(nc, es, "ln_o32_%d" % i, [128, 512], F32) for i in range(3)]
            o16 = [sb(nc, es, "ln_o16_%d" % i, [128, 512], BF16) for i in range(3)]
            b32, b16 = P.bufs_n("ln_o32_", 3), P.bufs_n("ln_o16_", 3)
        else:
            rows = [sb(nc, es, "ln_rows%d" % i, [128, D], F32) for i in range(2)]
            b_rows = P.bufs_n("ln_rows", 2)
            pst = [ps(nc, es, "ln_tps%d" % i, [128, 512]) for i in range(4)]
            b_pst = P.pbufs_n("ln_tps", 4)
        zsrc, zbuf = (C.S_z, C.B_z) if which == 0 else (C.S_z2, C.B_z2)
        zv = zsrc.rearrange("c p t -> p c t")
        pc = 0
        rc = 0
        for tt in range(T // 512):
            t0 = tt * 512
            for part in range(4):
                P.dma("sp", zt[:, part * 8:(part + 1) * 8, :], zv[:, part * 8:(part + 1) * 8, t0:t0 + 512], b_z, zbuf, True)
            ln_stats_run(C, S, zt, b_z)
            for fc in range(NCH):
                P.op("dve", "tensor_tensor", zt[:, fc, :], zt[:, fc, :], S["mean"][:], ALU.subtract,
                     reads=[b_z, S["b_st"]], writes=[b_z])
                P.op("dve", "tensor_tensor", zt[:, fc, :], zt[:, fc, :], S["rstd"][:], ALU.mult,
                     reads=[b_z, S["b_st"]], writes=[b_z])
                if which == 0:
                    i = (tt * NCH + fc) % 3
                    P.op("act", "activation", out=o32[i][:], in_=zt[:, fc, :], func=AF.Identity,
                         scale=vcol(C.vec_t, V_AG0, fc), bias=vcol(C.vec_t, V_AB0, fc),
                         reads=[b_z, C.B_const], writes=[b32[i]])
                    P.dma("sp", C.S_xT[fc, :, t0:t0 + 512], o32[i][:], b32[i], C.B_xT, False)
                    P.op("act", "activation", out=o16[i][:], in_=zt[:, fc, :], func=AF.Identity,
                         scale=vcol(C.vec_t, V_A2, fc), bias=vcol(C.vec_t, V_B2, fc),
                         reads=[b_z, C.B_const], writes=[b16[i]])
                    P.dma("sp", C.S_h2[fc, :, t0:t0 + 512], o16[i][:], b16[i], C.B_h2, False)
                else:
                    P.op("act", "activation", out=zt[:, fc, :], in_=zt[:, fc, :], func=AF.Identity,
                         scale=C.lng_t[:, NCH + fc:NCH + fc + 1], bias=C.lnb_t[:, NCH + fc:NCH + fc + 1],
                         reads=[b_z, C.B_const], writes=[b_z])
            if which == 1:
                for blk in range(4):
                    r = rc % 2
                    rc += 1
                    for g in range(NCH // 4):
                        pb = pc % 4
                        pc += 1
                        for i in range(4):
                            fc = g * 4 + i
                            P.op("pe", "transpose", pst[pb][:, i * 128:(i + 1) * 128], zt[:, fc, blk * 128:(blk + 1) * 128], C.ident_t[:],
                                 reads=[b_z, C.B_const], writes=[b_pst[pb]])
                        eng = "act" if g % 2 == 0 else "dve"
                        if eng == "act":
                            P.op("act", "activation", out=rows[r][:, g * 512:(g + 1) * 512], in_=pst[pb][:, :],
                                 func=AF.Copy, reads=[b_pst[pb]], writes=[b_rows[r]])
                        else:
                            P.op("dve", "tensor_copy", rows[r][:, g * 512:(g + 1) * 512], pst[pb][:, :],
                                 reads=[b_pst[pb]], writes=[b_rows[r]])
                    dst_ap, dst_buf = C.out_rows(t0 + blk * 128)
                    P.dma("sp", dst_ap, rows[r][:], b_rows[r], dst_buf, False)
        P.barrier()


def phase_conv(C):
    P, nc = C.P, C.nc
    W = HALO + T
    with ExitStack() as es:
        cw = sb(nc, es, "cv_w", [128, 16 * CONV_K], F32)
        cb = sb(nc, es, "cv_b", [128, 16], F32)
        b_cw = P.buf("cv_w")
        P.dma("sp", cw[:], C.conv_w[:, :], b_cw, None, True)
        P.dma("sp", cb[:], C.conv_b[:, :], b_cw, None, True)
        av = [sb(nc, es, "cv_av%d" % i, [128, W], F32) for i in range(2)]
        ag = [sb(nc, es, "cv_ag%d" % i, [128, W], F32) for i in range(2)]
        acc = [sb(nc, es, "cv_acc%d" % i, [128, T], F32) for i in range(2)]
        b_av, b_ag, b_acc = P.bufs_n("cv_av", 2), P.bufs_n("cv_ag", 2), P.bufs_n("cv_acc", 2)
        for cc in range(16):
            s = cc % 2
            eng = "dve"
            P.dma("sp", av[s][:], C.S_u32[cc, :, T - HALO:TE], b_av[s], C.B_u, True)
            P.dma("sp", ag[s][:], C.S_u32[16 + cc, :, T - HALO:TE], b_ag[s], C.B_u, True)
            P.op("act", "activation", out=ag[s][:], in_=ag[s][:], func=AF.Sigmoid, reads=[b_ag[s]], writes=[b_ag[s]])
            P.op(eng, "tensor_tensor", av[s][:], av[s][:], ag[s][:], ALU.mult, reads=[b_av[s], b_ag[s]], writes=[b_av[s]])
            P.op(eng, "tensor_scalar", av[s][:, 0:HALO], av[s][:, 0:HALO], C.flag_t[:, 0:1], None, ALU.mult,
                 reads=[b_av[s], C.B_const], writes=[b_av[s]])
            o0 = HALO - (CONV_K - 1)
            P.op(eng, "tensor_scalar", acc[s][:], av[s][:, o0:o0 + T], cw[:, cc * CONV_K:cc * CONV_K + 1], cb[:, cc:cc + 1],
                 ALU.mult, ALU.add, reads=[b_av[s], b_cw], writes=[b_acc[s]])
            for j in range(1, CONV_K):
                P.op(eng, "scalar_tensor_tensor", acc[s][:], av[s][:, o0 + j:o0 + j + T],
                     cw[:, cc * CONV_K + j:cc * CONV_K + j + 1], acc[s][:], ALU.mult, ALU.add,
                     reads=[b_av[s], b_cw, b_acc[s]], writes=[b_acc[s]])
            P.dma("sp", C.S_cv[cc, :, :], acc[s][:], b_acc[s], C.B_cv, False)
        P.barrier()
    with ExitStack() as es:
        g = sb(nc, es, "cl_g", [128, 16], F32)
        bb = sb(nc, es, "cl_b", [128, 16], F32)
        b_g = P.buf("cl_g")
        P.dma("sp", g[:], C.conv_g[:, :], b_g, None, True)
        P.dma("sp", bb[:], C.conv_bb[:, :], b_g, None, True)
        zt = sb(nc, es, "cl_z", [128, 16, 512], F32)
        b_z = P.buf("cl_z")
        S = ln_stats(C, es, "cl", zt, 16, MIX, b_z)
        o16 = [sb(nc, es, "cl_o%d" % i, [128, 512], BF16) for i in range(3)]
        b16 = P.bufs_n("cl_o", 3)
        zv = C.S_cv.rearrange("c p t -> p c t")
        for tt in range(T // 512):
            t0 = tt * 512
            for part in range(2):
                P.dma("sp", zt[:, part * 8:(part + 1) * 8, :], zv[:, part * 8:(part + 1) * 8, t0:t0 + 512], b_z, C.B_cv, True)
            ln_stats_run(C, S, zt, b_z)
            for cc in range(16):
                P.op("dve", "tensor_tensor", zt[:, cc, :], zt[:, cc, :], S["mean"][:], ALU.subtract,
                     reads=[b_z, S["b_st"]], writes=[b_z])
                P.op("dve", "tensor_tensor", zt[:, cc, :], zt[:, cc, :], S["rstd"][:], ALU.mult,
                     reads=[b_z, S["b_st"]], writes=[b_z])
                i = (tt * 16 + cc) % 3
                P.op("act", "activation", out=o16[i][:], in_=zt[:, cc, :], func=AF.Silu,
                     scale=g[:, cc:cc + 1], bias=bb[:, cc:cc + 1], reads=[b_z, b_g], writes=[b16[i]])
                P.dma("sp", C.S_mix[cc, :, t0:t0 + 512], o16[i][:], b16[i], C.B_mix, False)
        P.barrier()


def phase_attn(C):
    P, nc = C.P, C.nc
    scale = 128.0 ** -0.5
    DILS = (1, 4, 16)
    with ExitStack() as es:
        qT = sb(nc, es, "at_q", [128, T], BF16)
        kT = sb(nc, es, "at_k", [128, TE], BF16)
        vT = sb(nc, es, "at_v", [128, TE], BF16)
        bm = sb(nc, es, "at_bm", [128, 3, 256], F32)
        mF = sb(nc, es, "at_mF", [128, 3, 256], F32)
        Vt = sb(nc, es, "at_Vt", [128, 3, 32, 128], BF16)
        acc = sb(nc, es, "at_acc", [128, 2, T], F32)
        outb = sb(nc, es, "at_out", [128, T], BF16)
        pt = [sb(nc, es, "at_pt%d" % i, [128, 256], F32) for i in range(2)]
        ptm = [sb(nc, es, "at_ptm%d" % i, [128, 256], BF16) for i in range(2)]
        b_q, b_k, b_v, b_bm, b_mF, b_Vt, b_acc, b_out = (P.buf(n) for n in
                                                       ("at_q", "at_k", "at_v", "at_bm", "at_mF", "at_Vt", "at_acc", "at_out"))
        b_pt, b_ptm = P.bufs_n("at_pt", 2), P.bufs_n("at_ptm", 2)
        p_s = [ps(nc, es, "at_ps%d" % i, [128, 512])[:, 0:256].rearrange("p (a b) -> p a b", a=2) for i in range(2)]
        p_o = [ps(nc, es, "at_po%d" % i, [128, 512])[:, 0:256].rearrange("p (a b) -> p a b", a=2) for i in range(2)]
        p_v = [ps(nc, es, "at_pv%d" % i, [128, 512])[:, :].rearrange("p (a b) -> p a b", a=4) for i in range(2)]
        b_ps, b_po, b_pv = P.pbufs_n("at_ps", 2), P.pbufs_n("at_po", 2), P.pbufs_n("at_pv", 2)
        it = 0
        vc = 0
        for h in range(16):
            P.dma("sp", qT[:], C.S_u16[32 + h, :, T:TE], b_q, C.B_u, True)
            P.dma("sp", kT[:], C.S_u16[48 + h, :, :], b_k, C.B_u, True)
            P.dma("sp", vT[:], C.S_u16[64 + h, :, :], b_v, C.B_u, True)
            P.dma("sp", bm[:], C.biasmat[h].rearrange("p (d k) -> p d k", d=3), b_bm, None, True)
            P.op("act", "activation", out=bm[:], in_=bm[:], func=AF.Exp, reads=[b_bm], writes=[b_bm])
            P.op("dve", "tensor_scalar", mF[:, :, 0:128], bm[:, :, 0:128], C.flag_t[:, 0:1], None, ALU.mult,
                 reads=[b_bm, C.B_const], writes=[b_mF])
            P.op("dve", "tensor_copy", mF[:, :, 128:256], bm[:, :, 128:256], reads=[b_bm], writes=[b_mF])
            P.op("pool", "memset", acc[:], 0.0, writes=[b_acc])
            for di, dl in enumerate(DILS):
                for g in range(8):
                    pv = vc % 2
                    vc += 1
                    for i in range(4):
                        blk = g * 4 + i
                        n, r = blk // dl, blk % dl
                        base = 128 * dl * n + r
                        P.op("pe", "matmul", p_v[pv][:, i, :], vT[:, base:base + 127 * dl + 1:dl], C.identb_t[:],
                             start=True, stop=True, reads=[b_v, C.B_const], writes=[b_pv[pv]])
                    P.op("act", "activation", out=Vt[:, di, g * 4:(g + 1) * 4, :], in_=p_v[pv][:, :, :], func=AF.Copy,
                         reads=[b_pv[pv]], writes=[b_Vt])
            for di, dl in enumerate(DILS):
                nsb = 32 // dl
                for n in range(nsb // 2, nsb):
                    for r in range(dl):
                        blk_c = n * dl + r
                        blk_p = (n - 1) * dl + r
                        base_c = 128 * dl * n + r
                        base_p = 128 * dl * (n - 1) + r
                        qs = slice(base_c - T, base_c - T + 127 * dl + 1, dl)
                        i2 = it % 2
                        it += 1
                        P.op("pe", "matmul", p_s[i2][:, 0, :], kT[:, base_p:base_p + 127 * dl + 1:dl], qT[:, qs],
                             start=True, stop=True, reads=[b_k, b_q], writes=[b_ps[i2]])
                        P.op("pe", "matmul", p_s[i2][:, 1, :], kT[:, base_c:base_c + 127 * dl + 1:dl], qT[:, qs],
                             start=True, stop=True, reads=[b_k, b_q], writes=[b_ps[i2]])
                        P.op("act", "activation", out=pt[i2][:], in_=p_s[i2][:, :, :], func=AF.Exp, scale=scale,
                             reads=[b_ps[i2]], writes=[b_pt[i2]])
                        first = (n == nsb // 2)
                        msk = mF if first else bm
                        P.op("dve", "tensor_tensor", ptm[i2][:], pt[i2][:], msk[:, di, :], ALU.mult,
                             reads=[b_pt[i2], b_mF if first else b_bm], writes=[b_ptm[i2]])
                        P.op("pe", "matmul", p_o[i2][:, 0, :], Vt[:, di, blk_p, :], ptm[i2][:, 0:128],
                             start=True, stop=False, reads=[b_Vt, b_ptm[i2]], writes=[b_po[i2]])
                        P.op("pe", "matmul", p_o[i2][:, 0, :], Vt[:, di, blk_c, :], ptm[i2][:, 128:256],
                             start=False, stop=True, reads=[b_Vt, b_ptm[i2]], writes=[b_po[i2]])
                        P.op("pe", "matmul", p_o[i2][:, 1, :], C.onesb_t[:], ptm[i2][:, 0:128],
                             start=True, stop=False, reads=[C.B_const, b_ptm[i2]], writes=[b_po[i2]])
                        P.op("pe", "matmul", p_o[i2][:, 1, :], C.onesb_t[:], ptm[i2][:, 128:256],
                             start=False, stop=True, reads=[C.B_const, b_ptm[i2]], writes=[b_po[i2]])
                        P.op("dve", "tensor_tensor", acc[:, :, qs], acc[:, :, qs], p_o[i2][:, :, :], ALU.add,
                             reads=[b_acc, b_po[i2]], writes=[b_acc])
            P.op("dve", "reciprocal", acc[:, 1, :], acc[:, 1, :], reads=[b_acc], writes=[b_acc])
            P.op("dve", "tensor_tensor", outb[:], acc[:, 0, :], acc[:, 1, :], ALU.mult, reads=[b_acc], writes=[b_out])
            P.dma("sp", C.S_mix[16 + h, :, :], outb[:], b_out, C.B_mix, False)
        P.barrier()


def phase_hgrn(C):
    P, nc = C.P, C.nc
    NCK = TE // 64
    with ExitStack() as es:
        lbl = sb(nc, es, "hg_lbl", [128, 32], F32)
        lb = sb(nc, es, "hg_lb", [128, 16], F32)
        omlb = sb(nc, es, "hg_omlb", [128, 16], F32)
        ng = sb(nc, es, "hg_ng", [128, 16], F32)
        cm = sb(nc, es, "hg_cm", [64, 64], F32)
        b_c = P.buf("hg_const")
        P.dma("sp", lbl[:], C.lbl[:, :], b_c, None, True)
        P.dma("sp", ng[:], C.hng[:, :], b_c, None, True)
        P.dma("sp", cm[:], C.cmask[:, :], b_c, None, True)
        P.op("dve", "tensor_tensor", lb[:], lbl[:, 16:32], lbl[:, 0:16], ALU.subtract, reads=[b_c], writes=[b_c])
        P.op("act", "activation", out=lb[:], in_=lb[:], func=AF.Sigmoid, reads=[b_c], writes=[b_c])
        P.op("dve", "tensor_scalar", omlb[:], lb[:], -1.0, 1.0, ALU.mult, ALU.add, reads=[b_c], writes=[b_c])

        fT = sb(nc, es, "hg_f", [128, TE], F32)
        kk = sb(nc, es, "hg_kk", [128, TE], F32)
        bA = sb(nc, es, "hg_bA", [128, NCK, 64], F32)
        bB = sb(nc, es, "hg_bB", [128, NCK, 64], F32)
        ebl = sb(nc, es, "hg_ebl", [128, NCK], F32)
        qT = sb(nc, es, "hg_q", [128, T], F32)
        qe = sb(nc, es, "hg_qe", [128, T], BF16)
        ke = sb(nc, es, "hg_ke", [128, T], BF16)
        ke2 = sb(nc, es, "hg_ke2", [128, TE], BF16)
        iT = sb(nc, es, "hg_i", [128, TE], BF16)
        gT = sb(nc, es, "hg_g", [128, T], F32)
        tmp = sb(nc, es, "hg_tmp", [128, TE], F32)
        ke2tm = sb(nc, es, "hg_ke2tm", [64, NCK, 128], BF16)
        vtm = sb(nc, es, "hg_vtm", [64, NCK, 128], BF16)
        St = sb(nc, es, "hg_S", [128, 128], F32)
        Sb = sb(nc, es, "hg_Sb", [128, 128], BF16)
        oT = sb(nc, es, "hg_o", [128, T], F32)
        osq = sb(nc, es, "hg_osq", [128, T], F32)
        outb = sb(nc, es, "hg_out", [128, T], BF16)
        am = [sb(nc, es, "hg_am%d" % i, [64, 64], BF16) for i in range(2)]
        names = ("f", "kk", "bA", "bB", "ebl", "q", "qe", "ke", "ke2", "i", "g", "tmp", "ke2tm", "vtm", "S", "Sb", "o",
                 "osq", "out")
        B = {n: P.buf("hg_" + n) for n in names}
        b_am = P.bufs_n("hg_am", 2)
        p_t = [ps(nc, es, "hg_pt%d" % i, [128, 512])[0:64, :].rearrange("p (a b) -> p a b", a=4) for i in range(2)]
        pk = [ps(nc, es, "hg_pk%d" % i, [128, 512]) for i in range(2)]
        pks = [ps(nc, es, "hg_pks%d" % i, [128, 512]) for i in range(2)]
        p_S = [pks[i][:, 0:128] for i in range(2)]
        p_a = [pk[i][0:64, 128:192] for i in range(2)]
        p_o = [pk[i][:, 192:256] for i in range(2)]
        b_pt = P.pbufs_n("hg_pt", 2)
        b_pS = P.pbufs_n("hg_pks", 2)
        b_pa = P.pbufs_n("hg_pk", 2)
        b_po = b_pa
        Sb2 = [sb(nc, es, "hg_Sb2_%d" % i, [128, 128], BF16) for i in range(2)]
        b_Sb2 = P.bufs_n("hg_Sb2_", 2)
        p_n = ps(nc, es, "hg_pn", [128, 512])
        b_pn = P.pbuf("hg_pn")
        tcnt = 0
        for h in range(16):
            P.dma("sp", fT[:], C.S_u32[16 + h, :, :], B["f"], C.B_u, True)
            P.dma("sp", qT[:], C.S_u32[h, :, T:TE], B["q"], C.B_u, True)
            P.dma("sp", iT[:], C.S_u16[32 + h, :, :], B["i"], C.B_u, True)
            P.dma("sp", gT[:], C.S_u32[48 + h, :, T:TE], B["g"], C.B_u, True)
            P.op("act", "activation", out=fT[:], in_=fT[:], func=AF.Sigmoid, reads=[B["f"]], writes=[B["f"]])
            P.op("dve", "tensor_scalar", fT[:], fT[:], omlb[:, h:h + 1], lb[:, h:h + 1], ALU.mult, ALU.add,
                 reads=[B["f"], b_c], writes=[B["f"]])
            P.op("pool", "tensor_scalar", kk[:], fT[:], -1.0, 1.0, ALU.mult, ALU.add, reads=[B["f"]], writes=[B["kk"]])
            P.op("act", "activation", out=bA[:].rearrange("p c t -> p (c t)"), in_=fT[:], func=AF.Ln,
                 reads=[B["f"], B["kk"]], writes=[B["bA"]])
            src, dst, bs, bd = bA, bB, B["bA"], B["bB"]
            for sft in (1, 2, 4, 8, 16, 32):
                P.op("pool", "tensor_copy", dst[:, :, 0:sft], src[:, :, 0:sft], reads=[bs], writes=[bd])
                P.op("dve", "tensor_tensor", dst[:, :, sft:64], src[:, :, sft:64], src[:, :, 0:64 - sft], ALU.add,
                     reads=[bs], writes=[bd])
                src, dst, bs, bd = dst, src, bd, bs
            bcum, b_b = src, bs
            oth, b_oth = dst, bd
            P.op("act", "activation", out=ebl[:], in_=bcum[:, :, 63], func=AF.Exp, reads=[b_b], writes=[B["ebl"]])
            for c in range(NCK):
                P.op("act", "activation", out=tmp[:, c * 64:(c + 1) * 64], in_=bcum[:, c, :], func=AF.Exp, scale=-1.0,
                     bias=bcum[:, c, 63:64], reads=[b_b], writes=[B["tmp"]])
            P.op("dve", "tensor_tensor", ke2[:], tmp[:], kk[:], ALU.mult, reads=[B["tmp"], B["kk"]], writes=[B["ke2"]])
            bown = bcum[:, NCK // 2:NCK, :].rearrange("p c t -> p (c t)")
            P.op("act", "activation", out=qT[:], in_=qT[:], func=AF.Silu, reads=[B["q"]], writes=[B["q"]])
            P.op("act", "activation", out=oth[:, 0:NCK // 2, :].rearrange("p c t -> p (c t)"), in_=bown, func=AF.Exp,
                 reads=[b_b], writes=[b_oth])
            P.op("dve", "tensor_tensor", qe[:], qT[:], oth[:, 0:NCK // 2, :].rearrange("p c t -> p (c t)"), ALU.mult,
                 reads=[B["q"], b_oth], writes=[B["qe"]])
            P.op("act", "activation", out=oth[:, NCK // 2:NCK, :].rearrange("p c t -> p (c t)"), in_=bown, func=AF.Exp,
                 scale=-1.0, reads=[b_b], writes=[b_oth])
            P.op("dve", "tensor_tensor", ke[:], kk[:, T:TE], oth[:, NCK // 2:NCK, :].rearrange("p c t -> p (c t)"),
                 ALU.mult, reads=[B["kk"], b_oth], writes=[B["ke"]])
            P.op("act", "activation", out=gT[:], in_=gT[:], func=AF.Silu, reads=[B["g"]], writes=[B["g"]])
            for srcT, dstm, bsrc, bdst in ((ke2, ke2tm, B["ke2"], B["ke2tm"]), (iT, vtm, B["i"], B["vtm"])):
                for g4 in range(NCK // 4):
                    pi = tcnt % 2
                    tcnt += 1
                    for i in range(4):
                        c = g4 * 4 + i
                        P.op("pe", "matmul", p_t[pi][:, i, :], srcT[:, c * 64:(c + 1) * 64], C.identb_t[:],
                             start=True, stop=True, reads=[bsrc, C.B_const], writes=[b_pt[pi]])
                    P.op("act", "activation", out=dstm[:, g4 * 4:(g4 + 1) * 4, :], in_=p_t[pi][:, :, :], func=AF.Copy,
                         reads=[b_pt[pi]], writes=[bdst])
            H2 = NCK // 2
            P.op("pool", "memset", St[:], 0.0, writes=[B["S"]])
            for c in range(H2):
                i2 = c % 2
                P.op("pe", "matmul", p_S[i2], ke2tm[:, c, :], vtm[:, c, :], start=True, stop=True,
                     reads=[B["ke2tm"], B["vtm"]], writes=[b_pS[i2]])
                P.op("dve", "scalar_tensor_tensor", St[:], St[:], ebl[:, c:c + 1], p_S[i2], ALU.mult, ALU.add,
                     reads=[B["S"], B["ebl"], b_pS[i2]], writes=[B["S"]])
            P.op("dve", "tensor_scalar", St[:], St[:], C.flag_t[:, 0:1], None, ALU.mult,
                 reads=[B["S"], C.B_const], writes=[B["S"]])
            P.op("dve", "tensor_copy", Sb2[H2 % 2][:], St[:], reads=[B["S"]], writes=[b_Sb2[H2 % 2]])

            def front(c):
                j2 = c % 2
                cq = slice((c - H2) * 64, (c - H2 + 1) * 64)
                P.op("pe", "matmul", p_a[j2], ke[:, cq], qe[:, cq], start=True, stop=True,
                     reads=[B["ke"], B["qe"]], writes=[b_pa[j2]])
                if c < NCK - 1:
                    P.op("pe", "matmul", p_S[j2], ke2tm[:, c, :], vtm[:, c, :], start=True, stop=True,
                         reads=[B["ke2tm"], B["vtm"]], writes=[b_pS[j2]])

            def front_dve(c):
                j2 = c % 2
                P.op("dve", "tensor_tensor", am[j2][:], p_a[j2], cm[:], ALU.mult,
                     reads=[b_pa[j2], b_c], writes=[b_am[j2]])

            front(H2)
            front_dve(H2)
            for c in range(H2, NCK):
                i2 = c % 2
                cs_ = slice((c - H2) * 64, (c - H2 + 1) * 64)
                P.op("pe", "matmul", p_o[i2], Sb2[i2][:], qe[:, cs_], start=True, stop=False,
                     reads=[b_Sb2[i2], B["qe"]], writes=[b_po[i2]])
                P.op("pe", "matmul", p_o[i2], vtm[:, c, :], am[i2][:], start=False, stop=True,
                     reads=[B["vtm"], b_am[i2]], writes=[b_po[i2]])
                P.op("act", "activation", out=oT[:, cs_], in_=p_o[i2], func=AF.Copy,
                     reads=[b_po[i2]], writes=[B["o"]])
                if c < NCK - 1:
                    front(c + 1)
                    P.op("dve", "scalar_tensor_tensor", Sb2[(c + 1) % 2][:], St[:], ebl[:, c:c + 1], p_S[i2],
                         ALU.mult, ALU.add, reads=[B["S"], B["ebl"], b_pS[i2]], writes=[b_Sb2[(c + 1) % 2]])
                    P.op("dve", "scalar_tensor_tensor", St[:], St[:], ebl[:, c:c + 1], p_S[i2], ALU.mult, ALU.add,
                         reads=[B["S"], B["ebl"], b_pS[i2]], writes=[B["S"]])
                    front_dve(c + 1)
            P.op("act", "activation", out=osq[:], in_=oT[:], func=AF.Square, reads=[B["o"]], writes=[B["osq"]])
            for tt in range(T // 512):
                ts_ = slice(tt * 512, (tt + 1) * 512)
                P.op("pe", "matmul", p_n[:, :], C.ones_t[:], osq[:, ts_], start=True, stop=True,
                     reads=[B["osq"], C.B_const], writes=[b_pn])
                P.op("dve", "tensor_scalar", tmp[:, ts_], p_n[:, :], 1.0 / 128.0, EPS, ALU.mult, ALU.add,
                     reads=[b_pn], writes=[B["tmp"]])
            P.op("act", "activation", out=tmp[:, 0:T], in_=tmp[:, 0:T], func=AF.Sqrt, reads=[B["tmp"]], writes=[B["tmp"]])
            P.op("dve", "reciprocal", tmp[:, 0:T], tmp[:, 0:T], reads=[B["tmp"]], writes=[B["tmp"]])
            P.op("dve", "tensor_tensor", oT[:], oT[:], tmp[:, 0:T], ALU.mult, reads=[B["o"], B["tmp"]], writes=[B["o"]])
            P.op("dve", "scalar_tensor_tensor", outb[:], oT[:], ng[:, h:h + 1], gT[:], ALU.mult, ALU.mult,
                 reads=[B["o"], B["g"], b_c], writes=[B["out"]])
            P.dma("sp", C.S_mix[h, :, :], outb[:], B["out"], C.B_mix, False)
        P.barrier()


def phase_pool(C):
    P, nc = C.P, C.nc
    W = HALO + T
    with ExitStack() as es:
        ic = sb(nc, es, "pl_ic", [128, 4, T], F32)
        b_ic = P.buf("pl_ic")
        P.dma("sp", ic[:], C.invcnt.rearrange("p (g t) -> p g t", g=4), b_ic, None, True)
        p0 = [sb(nc, es, "pl_p%d" % i, [128, W], F32) for i in range(2)]
        sA = [sb(nc, es, "pl_a%d" % i, [128, W], F32) for i in range(2)]
        sB = [sb(nc, es, "pl_b%d" % i, [128, W], F32) for i in range(2)]
        o16 = [sb(nc, es, "pl_o%d" % i, [128, T], BF16) for i in range(2)]
        b_p, b_a, b_b, b_o = P.bufs_n("pl_p", 2), P.bufs_n("pl_a", 2), P.bufs_n("pl_b", 2), P.bufs_n("pl_o", 2)
        for cc in range(16):
            s = cc % 2
            gi = cc // 4
            eng = "dve"
            P.dma("sp", p0[s][:], C.S_u32[64 + cc, :, T - HALO:TE], b_p[s], C.B_u, True)
            P.op(eng, "tensor_scalar", p0[s][:, 0:HALO], p0[s][:, 0:HALO], C.flag_t[:, 0:1], None, ALU.mult,
                 reads=[b_p[s], C.B_const], writes=[b_p[s]])
            src, bsrc = p0[s], b_p[s]
            bufs = [(sA[s], b_a[s]), (sB[s], b_b[s])]
            sh = 1
            for k in range(gi + 1):
                dst, bdst = bufs[k % 2]
                P.op(eng, "tensor_tensor", dst[:, 16:W], src[:, 16:W], src[:, 16 - sh:W - sh], ALU.add,
                     reads=[bsrc], writes=[bdst])
                src, bsrc = dst, bdst
                sh *= 2
            P.op(eng, "tensor_tensor", src[:, HALO:W], src[:, HALO:W], ic[:, gi, :], ALU.mult,
                 reads=[bsrc, b_ic], writes=[bsrc])
            P.op(eng, "tensor_tensor", o16[s][:], src[:, HALO:W], p0[s][:, HALO:W], ALU.subtract,
                 reads=[bsrc, b_p[s]], writes=[b_o[s]])
            P.dma("sp", C.S_pl[cc, :, :], o16[s][:], b_o[s], C.B_pl, False)
        P.barrier()
    for g in range(4):
        epi = EpiStore(C, C.B_pl)
        with ExitStack() as es2:
            psc = sb(nc, es2, "pl_sc%d" % g, [128, 16], F32)
            b_sc = P.buf("pl_sc")
            P.dma("sp", psc[:], C.pool_s[:, :], b_sc, None, True)

            def ep(cc, t0, pap, pbuf, g=g, epi=epi, psc=psc, b_sc=b_sc):
                i = epi.n % epi.ns
                epi.n += 1
                ch = 4 * g + cc
                P.op("act", "activation", out=epi.st16[i][:], in_=pap, func=AF.Identity, scale=psc[:, ch:ch + 1],
                     reads=[pbuf, b_sc], writes=[epi.b16[i]])
                P.dma("sp", C.S_mix[16 + ch, :, t0:t0 + 512], epi.st16[i][:], epi.b16[i], C.B_mix, False)

            ep.setup = epi.setup
            ep.act_buf = C.B_pl
            allc = set(range(4))
            gemm(C, "plg%d" % g, C.pool_w[g], 512, 512, C.S_pl[4 * g:4 * g + 4], [0, 1024], 2, lambda t0: allc, ep)


def pf(v, n):
    return np.ascontiguousarray(np.asarray(v, np.float32).reshape(n, 128).T)


def t5_bucket_np(dist):
    max_exact = 16
    nf = np.maximum(dist, 1).astype(np.float32)
    large = max_exact + (np.log(nf / np.float32(max_exact)) / np.float32(np.log(2048 / max_exact))
                         * np.float32(32 - max_exact)).astype(np.int32)
    large = np.minimum(large, 31)
    return np.where(dist < max_exact, dist, large)


def make_biasmat(rel_bias):
    k = np.arange(128)[:, None, None]
    kb = np.arange(2)[None, :, None]
    q = np.arange(128)[None, None, :]
    step = q + 128 - (kb * 128 + k)
    valid = (step >= 0) & (step <= 128)
    out = np.full((16, 128, 3, 2, 128), -30000.0, np.float32)
    for di, dl in enumerate((1, 4, 16)):
        bucket = t5_bucket_np(np.clip(step, 0, 128) * dl)
        for h in range(16):
            vals = rel_bias[bucket, h]
            out[h, :, di] = np.where(valid, vals, np.float32(-30000.0))
    return out.reshape(16, 128, 3 * 256)


_NC_CACHE = {}
_DECL = {}


def nc_declared(nc):
    return _DECLS.get(id(nc), set())


_DECLS = {}


def layer_inputs(layer, inp, sfx):
    l = layer
    sh = {
        "w_in" + sfx: np.ascontiguousarray(inp["w_in"][l]),
        "w_out" + sfx: np.ascontiguousarray(inp["w_out"][l]),
        "w1" + sfx: np.ascontiguousarray(inp["mlp_w1"][l]),
        "w2" + sfx: np.ascontiguousarray(inp["mlp_w2"][l]),
        "ln_g" + sfx: pf(inp["ln_g"][l].reshape(-1), 2 * NCH),
        "ln_b" + sfx: pf(inp["ln_b"][l].reshape(-1), 2 * NCH),
    }
    if layer == 0:
        cw = np.asarray(inp["conv_w"][0], np.float32)
        sh["conv_w"] = np.ascontiguousarray(cw.T.reshape(16, 128, CONV_K).transpose(1, 0, 2).reshape(128, 16 * CONV_K))
        sh["conv_b"] = pf(inp["conv_b"][0], 16)
        sh["conv_g"] = pf(inp["conv_ln_g"][0], 16)
        sh["conv_bb"] = pf(inp["conv_ln_b"][0], 16)
        sh["biasmat"] = make_biasmat(np.asarray(inp["rel_bias"], np.float32))
    else:
        sh["lbl"] = pf(np.asarray(inp["hgrn_lb_logits"], np.float32).reshape(-1), 32)
        sh["hng"] = pf(inp["hgrn_norm_g"][0], 16)
        sh["pool_w"] = np.ascontiguousarray(inp["pool_w"][0])
        sh["pool_s"] = pf(inp["pool_scale"][0], 16)
        sh["cmask"] = np.triu(np.ones((64, 64), np.float32))
    return sh


def run_prog(layers, xfull, inp, phases=None, ncores=8):
    key = (tuple(layers), tuple(phases) if phases else None)
    if key not in _NC_CACHE:
        _NC_CACHE[key] = build_prog(tuple(layers), phases)
    nc = _NC_CACHE[key]
    fused = len(layers) > 1
    shared = {"ident": np.eye(128, dtype=np.float32)}
    for l in layers:
        shared.update(layer_inputs(l, inp, str(l) if fused else ""))
    in_maps = []
    for core in range(8):
        b, half = core // 2, core % 2
        xe = np.zeros((TE, D), np.float32)
        if half == 1:
            xe[:] = xfull[b]
        else:
            xe[T:] = xfull[b, :T]
        m = dict(shared)
        m["xe"] = xe
        m["flag"] = np.full((128, 1), float(half), np.float32)
        m["c_pf"] = pf(inp["c"][b], NCH)
        for l in layers:
            sfx = str(l) if fused else ""
            hw = 3 * D
            m["ada_w" + sfx] = np.ascontiguousarray(inp["ada_w"][l][:, half * hw:(half + 1) * hw])
            m["ada_b" + sfx] = np.ascontiguousarray(pf(inp["ada_b"][l], 6 * NCH)[:, half * 3 * NCH:(half + 1) * 3 * NCH])
        if 1 in layers:
            tabs = np.arange(T) + half * T
            ic = np.stack([1.0 / np.minimum(tabs + 1, w).astype(np.float32) for w in (2, 4, 8, 16)]).astype(np.float32)
            m["invcnt"] = np.ascontiguousarray(np.broadcast_to(ic.reshape(1, 4 * T), (128, 4 * T)))
        in_maps.append(m)
    in_maps = [{k: v for k, v in m.items() if k in nc_declared(nc)} for m in in_maps[:ncores]]
    res = run_bass_kernel_spmd(nc, in_maps, core_ids=list(range(ncores)))
    LAST_RES[0] = res
    if ncores < 8 or (phases is not None and "ln1" not in phases):
        return None
    out = np.empty((NB, SEQ, D), np.float32)
    for core in range(8):
        b, half = core // 2, core % 2
        out[b, half * T:(half + 1) * T] = res.results[core]["y"]
    return out


def run_layer(layer, xfull, inp, phases=None, ncores=8):
    return run_prog((layer,), xfull, inp, phases, ncores)


def kernel(**inputs):
    inp = {k: np.asarray(v) for k, v in inputs.items()}
    x = np.asarray(inp["x"], np.float32)
    return run_prog((0, 1), x, inp)
```
